# Optimizing a Trainium2 kernel written in Bass

```python
import math
import jax
import jax.numpy as jnp
from jax import lax
import numpy as np

D_MODEL = 2048
BATCH = 4
SEQ = 4096
DEPTH = 4
DEC_BATCH = 8
DEC_SEQ = 4096
PAST_LEN = 128

HEAD_DIM = 128
N_HEADS = D_MODEL // HEAD_DIM
A_Q_HEADS = N_HEADS // 4
A_KV_HEADS = A_Q_HEADS // 2
B_PAIRS = ((128, 1), (512, 4), (2048, 16))
B_HEADS_PER_PAIR = (N_HEADS - A_Q_HEADS) // 2 // len(B_PAIRS)
B_HEADS = B_HEADS_PER_PAIR * len(B_PAIRS)
C_HEADS = N_HEADS - A_Q_HEADS - B_HEADS
A_QW = A_Q_HEADS * HEAD_DIM
A_KVW = A_KV_HEADS * HEAD_DIM
B_W = B_HEADS * HEAD_DIM
C_W = C_HEADS * HEAD_DIM
MIX_W = A_QW + B_W + C_W
IN_W = A_QW + 2 * A_KVW + 3 * B_W + 3 * C_W
GRID_W = 64
NA_ROWS = 8
NA_COLS = 16
NA_QCOLS = 16
NA_KCOLS = 32
Q_BLOCK = 128
D_FF = 4 * D_MODEL
ROPE_THETA = 10000.0
EPS = 1e-6
NEG = -1e30
DEEPNORM_ALPHA = (2 * DEPTH) ** 0.25
DEEPNORM_BETA = (8 * DEPTH) ** -0.25

kernel_name = 'hybrid_bidir_encoder_gqa_dilated_natten'


def _rms_unit(x):
    xf = x.astype(jnp.float32)
    y = xf * lax.rsqrt(jnp.mean(xf * xf, axis=-1, keepdims=True) + EPS)
    return y.astype(x.dtype)


def _layernorm(x, g, b):
    xf = x.astype(jnp.float32)
    mu = jnp.mean(xf, axis=-1, keepdims=True)
    var = jnp.mean(jnp.square(xf - mu), axis=-1, keepdims=True)
    y = (xf - mu) * lax.rsqrt(var + EPS) * g.astype(jnp.float32) + b.astype(jnp.float32)
    return y.astype(x.dtype)


def _rope_angles(pos, dim):
    inv = ROPE_THETA ** (-(jnp.arange(dim // 2, dtype=jnp.float32) * 2.0 / dim))
    return pos.astype(jnp.float32)[:, None] * inv[None, :]


def _rope(x, ang):
    n = x.shape[-1] // 2
    xf = x.astype(jnp.float32)
    x1, x2 = xf[..., :n], xf[..., n:]
    c = jnp.cos(ang)[None, :, None, :]
    s = jnp.sin(ang)[None, :, None, :]
    return jnp.concatenate([x1 * c - x2 * s, x2 * c + x1 * s], axis=-1).astype(x.dtype)


def _axial_rope(x, row_ang, col_ang):
    half = HEAD_DIM // 2
    return jnp.concatenate([_rope(x[..., :half], row_ang), _rope(x[..., half:], col_ang)], axis=-1)


def _global_gqa(q, k, v):
    Bsz, S = q.shape[:2]
    rep = A_Q_HEADS // A_KV_HEADS
    scale = HEAD_DIM ** -0.5
    qb = q.reshape(Bsz, S // Q_BLOCK, Q_BLOCK, A_KV_HEADS, rep, HEAD_DIM)
    qb = jnp.moveaxis(qb, 1, 0)

    def block(qi):
        s = jnp.einsum('bqgrd,bkgd->bgrqk', qi, k, preferred_element_type=jnp.float32) * scale
        p = jax.nn.softmax(s, axis=-1)
        return jnp.einsum('bgrqk,bkgd->bqgrd', p.astype(v.dtype), v)

    o = lax.map(block, qb)
    return jnp.moveaxis(o, 0, 1).reshape(Bsz, S, A_QW)


def _dilated_window_attention(q, k, v, window, dilation):
    Bsz, S, H, D = q.shape
    n_side = (window // 2) // dilation
    L = S // dilation

    def strided(z):
        return z.reshape(Bsz, L, dilation, H, D).transpose(0, 2, 1, 3, 4)

    qs, ks, vs = strided(q), strided(k), strided(v)
    blk = math.gcd(L, Q_BLOCK)
    nb = L // blk
    kw = blk + 2 * n_side
    pad = ((0, 0), (0, 0), (n_side, n_side), (0, 0), (0, 0))
    kp = jnp.pad(ks, pad)
    vp = jnp.pad(vs, pad)
    starts = jnp.arange(nb) * blk
    idx = starts[:, None] + jnp.arange(kw)[None, :]
    kg = kp[:, :, idx]
    vg = vp[:, :, idx]
    qb = qs.reshape(Bsz, dilation, nb, blk, H, D)
    s = jnp.einsum('bznqhd,bznkhd->bznhqk', qb, kg, preferred_element_type=jnp.float32) * (HEAD_DIM ** -0.5)
    qi = jnp.arange(blk)[:, None]
    kj = jnp.arange(kw)[None, :]
    rel = kj - qi
    kpos = starts[:, None, None] + kj[None] - n_side
    valid = (rel >= 0)[None] & (rel <= 2 * n_side)[None] & (kpos >= 0) & (kpos < L)
    s = jnp.where(valid[None, None, :, None], s, NEG)
    lse = jax.nn.logsumexp(s, axis=-1)
    p = jnp.exp(s - lse[..., None])
    o = jnp.einsum('bznhqk,bznkhd->bznqhd', p.astype(v.dtype), vg)
    o = o.reshape(Bsz, dilation, L, H, D).transpose(0, 2, 1, 3, 4).reshape(Bsz, S, H, D)
    lse = lse.transpose(0, 1, 2, 4, 3).reshape(Bsz, dilation, L, H).transpose(0, 2, 1, 3).reshape(Bsz, S, H)
    return o, lse


def _dilated_mixture(q, k, v):
    Bsz, S = q.shape[:2]
    shp = (Bsz, S, len(B_PAIRS), B_HEADS_PER_PAIR, HEAD_DIM)
    q, k, v = q.reshape(shp), k.reshape(shp), v.reshape(shp)
    outs, lses = [], []
    for g, (window, dilation) in enumerate(B_PAIRS):
        o, lse = _dilated_window_attention(q[:, :, g], k[:, :, g], v[:, :, g], window, dilation)
        outs.append(o)
        lses.append(lse)
    o = jnp.stack(outs, axis=2)
    alpha = jax.nn.softmax(jnp.stack(lses, axis=2), axis=2)
    return (o.astype(jnp.float32) * alpha[..., None]).astype(q.dtype).reshape(Bsz, S, B_W)


def _neighbourhood_attention(q, k, v, rel_bias):
    Bsz, S, H, D = q.shape
    rows = S // GRID_W
    win_r = min(NA_ROWS, rows)
    n_cb = GRID_W // NA_QCOLS
    qg = q.reshape(Bsz, rows, n_cb, NA_QCOLS, H, D)
    kg = k.reshape(Bsz, rows, GRID_W, H, D)
    vg = v.reshape(Bsz, rows, GRID_W, H, D)
    cb = jnp.arange(n_cb)
    kc0 = jnp.clip(cb * NA_QCOLS - NA_COLS // 2, 0, GRID_W - NA_KCOLS)
    kcols = kc0[:, None] + jnp.arange(NA_KCOLS)[None, :]
    qcols = cb[:, None] * NA_QCOLS + jnp.arange(NA_QCOLS)[None, :]
    sc = jnp.clip(qcols - NA_COLS // 2, 0, GRID_W - NA_COLS)
    col_ok = (kcols[:, None, :] >= sc[..., None]) & (kcols[:, None, :] < sc[..., None] + NA_COLS)
    dcol = jnp.clip(kcols[:, None, :] - qcols[..., None] + NA_COLS - 1, 0, 2 * NA_COLS - 2)
    scale = HEAD_DIM ** -0.5

    def row_block(r):
        r0 = jnp.clip(r - win_r // 2, 0, rows - win_r)
        kr = lax.dynamic_slice_in_dim(kg, r0, win_r, axis=1)[:, :, kcols]
        vr = lax.dynamic_slice_in_dim(vg, r0, win_r, axis=1)[:, :, kcols]
        qr = lax.dynamic_index_in_dim(qg, r, axis=1, keepdims=False)
        s = jnp.einsum('bcqhd,bjckhd->bhcqjk', qr, kr, preferred_element_type=jnp.float32) * scale
        drow = r0 + jnp.arange(win_r) - r + NA_ROWS - 1
        bias = rel_bias[:, drow[None, None, :, None], dcol[:, :, None, :]]
        s = s + bias.astype(jnp.float32)[None]
        s = jnp.where(col_ok[None, None, :, :, None, :], s, NEG)
        p = jax.nn.softmax(s.reshape(s.shape[:4] + (win_r * NA_KCOLS,)), axis=-1).reshape(s.shape)
        return jnp.einsum('bhcqjk,bjckhd->bcqhd', p.astype(v.dtype), vr)

    o = lax.map(row_block, jnp.arange(rows))
    return jnp.moveaxis(o, 0, 1).reshape(Bsz, S, C_W)


def _layer(x, w_in, a_q_gain, a_k_gain, c_rel_bias, mix_gain, w_out, ln1_g, ln1_b, w_ff1, w_ff2, ln2_g, ln2_b):
    Bsz, S, _ = x.shape
    t = jnp.arange(S)
    sizes = [A_QW, A_KVW, A_KVW, B_W, B_W, B_W, C_W, C_W, C_W]
    splits = [int(c) for c in np.cumsum(sizes)[:-1]]
    proj = jnp.einsum('bsd,de->bse', x, w_in)
    aq, ak, av, bq, bk, bv, cq, ck, cv = jnp.split(proj, splits, axis=-1)

    def heads(z):
        return z.reshape(Bsz, S, -1, HEAD_DIM)

    row_ang = _rope_angles(t // GRID_W, HEAD_DIM // 2)
    col_ang = _rope_angles(t % GRID_W, HEAD_DIM // 2)
    qa = _axial_rope(_rms_unit(heads(aq)) * a_q_gain, row_ang, col_ang)
    ka = _axial_rope(_rms_unit(heads(ak)) * a_k_gain, row_ang, col_ang)
    o_a = _global_gqa(qa, ka, heads(av))
    ang = _rope_angles(t, HEAD_DIM)
    o_b = _dilated_mixture(_rope(heads(bq), ang), _rope(heads(bk), ang), heads(bv))
    o_c = _neighbourhood_attention(heads(cq), heads(ck), heads(cv), c_rel_bias)

    mixed = jnp.concatenate([_rms_unit(o_a), _rms_unit(o_b), _rms_unit(o_c)], axis=-1) * mix_gain
    y = jnp.einsum('bse,ed->bsd', mixed, w_out)
    x = _layernorm(DEEPNORM_ALPHA * x + y, ln1_g, ln1_b)
    h = jnp.einsum('bsf,fd->bsd', jnp.square(jax.nn.relu(jnp.einsum('bsd,df->bsf', x, w_ff1))), w_ff2)
    return _layernorm(DEEPNORM_ALPHA * x + h, ln2_g, ln2_b)


def _trunk(x, w_in, a_q_gain, a_k_gain, c_rel_bias, mix_gain, w_out, ln1_g, ln1_b, w_ff1, w_ff2, ln2_g, ln2_b):
    for l in range(DEPTH):
        x = _layer(x, w_in[l], a_q_gain[l], a_k_gain[l], c_rel_bias[l], mix_gain[l], w_out[l],
                   ln1_g[l], ln1_b[l], w_ff1[l], w_ff2[l], ln2_g[l], ln2_b[l])
    return x


def setup_inputs(seed: int = 0) -> dict:
    key = jax.random.key(seed)
    ks = jax.random.split(key, 15)
    nrm = jax.random.normal
    f32 = jnp.float32
    return {
        'x_prompt': nrm(ks[0], (BATCH, SEQ, D_MODEL), f32),
        'x_sample': nrm(ks[1], (DEC_BATCH, DEC_SEQ, D_MODEL), f32),
        'w_in': nrm(ks[2], (DEPTH, D_MODEL, IN_W), f32) * D_MODEL ** -0.5,
        'a_q_gain': 1.0 + 0.02 * nrm(ks[3], (DEPTH, HEAD_DIM), f32),
        'a_k_gain': 1.0 + 0.02 * nrm(ks[4], (DEPTH, HEAD_DIM), f32),
        'c_rel_bias': 0.02 * nrm(ks[5], (DEPTH, C_HEADS, 2 * NA_ROWS - 1, 2 * NA_COLS - 1), f32),
        'mix_gain': 1.0 + 0.02 * nrm(ks[6], (DEPTH, MIX_W), f32),
        'w_out': nrm(ks[7], (DEPTH, MIX_W, D_MODEL), f32) * (MIX_W ** -0.5 * DEEPNORM_BETA),
        'ln1_g': 1.0 + 0.02 * nrm(ks[8], (DEPTH, D_MODEL), f32),
        'ln1_b': 0.02 * nrm(ks[9], (DEPTH, D_MODEL), f32),
        'w_ff1': nrm(ks[10], (DEPTH, D_MODEL, D_FF), f32) * D_MODEL ** -0.5,
        'w_ff2': nrm(ks[11], (DEPTH, D_FF, D_MODEL), f32) * (D_FF ** -0.5 * DEEPNORM_BETA),
        'ln2_g': 1.0 + 0.02 * nrm(ks[12], (DEPTH, D_MODEL), f32),
        'ln2_b': 0.02 * nrm(ks[13], (DEPTH, D_MODEL), f32),
    }


def reference(x_prompt, x_sample, w_in, a_q_gain, a_k_gain, c_rel_bias, mix_gain, w_out,
              ln1_g, ln1_b, w_ff1, w_ff2, ln2_g, ln2_b):
    y_prompt = _trunk(x_prompt, w_in, a_q_gain, a_k_gain, c_rel_bias, mix_gain, w_out,
                      ln1_g, ln1_b, w_ff1, w_ff2, ln2_g, ln2_b)
    y_sample = _trunk(x_sample, w_in, a_q_gain, a_k_gain, c_rel_bias, mix_gain, w_out,
                      ln1_g, ln1_b, w_ff1, w_ff2, ln2_g, ln2_b)
    return (y_prompt, y_sample)
```

```python
import math
from contextlib import ExitStack

import numpy as np
import concourse.bass as bass
import concourse.mybir as mybir
from concourse.bass_utils import run_bass_kernel_spmd

F32 = mybir.dt.float32
BF16 = mybir.dt.bfloat16
ALU = mybir.AluOpType
AF = mybir.ActivationFunctionType

D_MODEL = 2048
SEQ = 4096
DEPTH = 4
IN_W = 5632
D_FF = 8192
EPS = 1e-6
ALPHA = (2 * DEPTH) ** 0.25
SCALE = 128 ** -0.5
NEGM = -30000.0
NDSEM = 40
B_DIL = (1, 4, 16)
DBG = {"nt": 8, "cast": True, "ptypes": "ABCV"}

CH_TYPES = (["Aq"] * 4 + ["Ak"] * 2 + ["Av"] * 2 + ["Bq"] * 6 + ["Bk"] * 6 + ["Bv"] * 6
            + ["Cq"] * 6 + ["Ck"] * 6 + ["Cv"] * 6)
assert len(CH_TYPES) == 44


def _chunk_dest():
    q = k = v = 0
    out = []
    for t in CH_TYPES:
        if t[1] == "q":
            out.append(("q", q)); q += 1
        elif t[1] == "k":
            out.append(("k", k)); k += 1
        else:
            out.append(("v", v)); v += 1
    return out


CH_DEST = _chunk_dest()


class Buf:
    __slots__ = ("w", "r", "excl")

    def __init__(self, excl=False):
        self.w = None
        self.r = {}
        self.excl = excl


class S:
    def __init__(self, nc, st):
        self.nc = nc
        self.names = ["pe", "act", "dve", "pool", "sp"]
        self.sems = []
        self.val = []
        self.esem = {}
        for e in self.names:
            self.esem[e] = self._newsem(st, "e_" + e)
        self.dsem = [self._newsem(st, "d%d" % i) for i in range(NDSEM)]
        self.dq = {"sp": self.dsem[:24], "pool": self.dsem[24:]}
        self.drr = {"sp": 0, "pool": 0}
        self.waited = {e: {} for e in self.names}
        self.ops = {e: [] for e in self.names}
        self.pend_r = []
        self.pend_w = []

    def _newsem(self, st, name):
        h = st.enter_context(self.nc.semaphore(name))
        self.sems.append(h)
        self.val.append(0)
        return len(self.sems) - 1

    def _deps(self, eng, reads, writes, extra=()):
        deps = {}

        def add(s, v):
            if deps.get(s, 0) < v:
                deps[s] = v
        for b in reads:
            if b.w is not None:
                add(*b.w)
            if b.excl:
                for s, v in b.r.items():
                    if s != self.esem.get(eng):
                        add(s, v)
        for b in writes:
            if b.w is not None:
                add(*b.w)
            for s, v in b.r.items():
                add(s, v)
        for s, v in extra:
            add(s, v)
        w = self.waited[eng]
        for s, v in deps.items():
            if eng == "pe" and s == self.esem["pe"]:
                continue
            if w.get(s, 0) < v:
                self.ops[eng].append(("w", s, v))
                w[s] = v

    def op(self, eng, meth, *args, reads=(), writes=(), inc=True, **kw):
        self._deps(eng, reads, writes)
        if eng == "pe" and not inc:
            self.ops[eng].append(("o", meth, args, kw, None))
            self.pend_r.extend(reads)
            self.pend_w.extend(writes)
            return
        s = self.esem[eng]
        self.val[s] += 1
        ev = (s, self.val[s])
        self.ops[eng].append(("o", meth, args, kw, s))
        rl, wl = list(reads), list(writes)
        if eng == "pe":
            rl += self.pend_r
            wl += self.pend_w
            self.pend_r = []
            self.pend_w = []
        for b in wl:
            b.w = ev
            b.r = {}
        for b in rl:
            if b.r.get(s, 0) < ev[1]:
                b.r[s] = ev[1]

    def dma(self, q, out, in_, reads=(), writes=()):
        pool_ = self.dq[q]
        s = pool_[self.drr[q]]
        self.drr[q] = (self.drr[q] + 1) % len(pool_)
        extra = [(s, self.val[s])] if self.val[s] > 0 else []
        self._deps(q, reads, writes, extra)
        self.val[s] += 16
        ev = (s, self.val[s])
        self.ops[q].append(("d", out, in_, s))
        for b in writes:
            b.w = ev
            b.r = {}
        for b in reads:
            if b.r.get(s, 0) < ev[1]:
                b.r[s] = ev[1]

    def barrier(self):
        assert not self.pend_r and not self.pend_w
        for e in self.names:
            w = self.waited[e]
            for s in range(len(self.sems)):
                if self.val[s] > w.get(s, 0):
                    self.ops[e].append(("w", s, self.val[s]))
                    w[s] = self.val[s]

    def replay(self, name, eng):
        sems = self.sems
        for it in self.ops[name]:
            k = it[0]
            if k == "w":
                eng.wait_ge(sems[it[1]], it[2])
            elif k == "o":
                ins = getattr(eng, it[1])(*it[2], **it[3])
                if it[4] is not None:
                    ins.then_inc(sems[it[4]], 1)
            else:
                eng.dma_start(out=it[1], in_=it[2]).then_inc(sems[it[3]], 16)


class Arena:
    def __init__(self, ap_f32, nbytes):
        self.ap = ap_f32
        self.n = nbytes
        self.off = 0

    def reset(self):
        self.off = 0

    def f32(self, cols):
        o = (self.off + 63) // 64 * 64
        self.off = o + cols * 4
        assert self.off <= self.n, ("arena overflow", self.off)
        return self.ap[:, o // 4:o // 4 + cols]

    def bf16(self, cols):
        o = (self.off + 63) // 64 * 64
        nb = (cols * 2 + 3) // 4 * 4
        self.off = o + nb
        assert self.off <= self.n, ("arena overflow", self.off)
        return self.ap[:, o // 4:o // 4 + nb // 4].bitcast(BF16)[:, 0:cols]


def dap(t, offset, ap):
    return bass.AP(tensor=t.tensor, offset=offset, ap=[list(x) for x in ap])


CF_IDENT, CF_ONES, CF_ONESM, CF_RTA, CF_RTB, CF_J, CF_CMASK, CF_BM, CF_OC3 = (
    0, 128, 256, 384, 512, 640, 704, 768, 768 + 768)
CF_EPS = CF_OC3 + 3
CF_N = CF_OC3 + 4


def host_consts():
    cf = np.zeros((128, CF_N), np.float32)
    cf[:, CF_IDENT:CF_IDENT + 128] = np.eye(128)
    cf[:, CF_ONES:CF_ONES + 128] = 1.0
    cf[:, CF_ONESM:CF_ONESM + 128] = 1.0 / 128.0

    def rt(segs):
        R = np.zeros((128, 128), np.float32)
        for (s0, n) in segs:
            for i in range(n):
                R[s0 + i, s0 + i + n] = -1.0
                R[s0 + i + n, s0 + i] = 1.0
        return R.T.copy()
    cf[:, CF_RTA:CF_RTA + 128] = rt([(0, 32), (64, 32)])
    cf[:, CF_RTB:CF_RTB + 128] = rt([(0, 64)])
    J = np.zeros((64, 64), np.float32)
    for i in range(64):
        J[i, 63 - i] = 1.0
    cf[0:64, CF_J:CF_J + 64] = J
    cm = np.full((64, 64), NEGM, np.float32)
    for qc in range(64):
        sc = min(max(qc - 8, 0), 48)
        cm[sc:sc + 16, qc] = 0.0
    cf[0:64, CF_CMASK:CF_CMASK + 64] = cm
    cf[64:128, CF_CMASK:CF_CMASK + 64] = cm
    j = np.arange(128)[:, None]
    qi = np.arange(128)[None, :]
    mn = np.full((128, 256), NEGM, np.float32)
    mn[:, 0:128][j <= qi] = 0.0
    mn[:, 128:256][j >= qi] = 0.0
    mf = mn.copy(); mf[0:64, :] = NEGM
    ml = mn.copy(); ml[64:128, :] = NEGM
    cf[:, CF_BM:CF_BM + 256] = mn
    cf[:, CF_BM + 256:CF_BM + 512] = mf
    cf[:, CF_BM + 512:CF_BM + 768] = ml
    cf[:, CF_OC3 + 0] = 1.0 / 512.0
    cf[:, CF_OC3 + 1] = 1.0 / 768.0
    cf[:, CF_OC3 + 2] = 1.0 / 768.0
    cf[:, CF_EPS] = EPS
    t = np.arange(SEQ)
    theta = np.float32(10000.0)

    def angles(pos, dim):
        inv = (theta ** (-(np.arange(dim // 2, dtype=np.float32) * np.float32(2.0) / np.float32(dim)))).astype(np.float32)
        return (pos.astype(np.float32)[:, None] * inv[None, :]).astype(np.float32)
    ra = angles(t // 64, 64)
    ca = angles(t % 64, 64)
    ab = angles(t, 128)
    angA = np.concatenate([ra, ra, ca, ca], axis=1)
    angB = np.concatenate([ab, ab], axis=1)
    rope = np.stack([np.cos(angA).T, np.sin(angA).T, np.cos(angB).T, np.sin(angB).T]).astype(np.float32)
    return cf, np.ascontiguousarray(rope)


def build(n_slots=2, layers=(0, 1, 2, 3), phases="PABCOF", debug=False):
    nc = bass.Bass("TRN2", target_bir_lowering=False)
    st = ExitStack()
    with st:
        def din(name, shape):
            return nc.dram_tensor(name, list(shape), F32, kind="ExternalInput").ap()

        skind = "ExternalOutput" if debug else "Internal"

        def dscr(name, shape, dt):
            return nc.dram_tensor(name, list(shape), dt, kind=skind).ap()

        x_in = din("x", (n_slots, SEQ, D_MODEL))
        w_in = din("w_in", (DEPTH, D_MODEL, IN_W))
        w_out = din("w_out", (DEPTH, D_MODEL, D_MODEL))
        w_ff1 = din("w_ff1", (DEPTH, D_MODEL, D_FF))
        w_ff2 = din("w_ff2", (DEPTH, D_FF, D_MODEL))
        aqg = din("a_q_gain", (DEPTH, 128))
        akg = din("a_k_gain", (DEPTH, 128))
        crb = din("c_rel_bias", (DEPTH, 6, 15, 31))
        mixg = din("mix_gain", (DEPTH, D_MODEL))
        ln1g = din("ln1_g", (DEPTH, D_MODEL))
        ln1b = din("ln1_b", (DEPTH, D_MODEL))
        ln2g = din("ln2_g", (DEPTH, D_MODEL))
        ln2b = din("ln2_b", (DEPTH, D_MODEL))
        cf_in = din("cf", (128, CF_N))
        rope_in = din("rope", (4, 128, SEQ))
        y_out = nc.dram_tensor("y", [n_slots, SEQ, D_MODEL], F32, kind="ExternalOutput").ap()

        WBi = nc.dram_tensor("WBi", [DEPTH, D_MODEL, IN_W], BF16, kind="Internal").ap()
        WBo = nc.dram_tensor("WBo", [DEPTH, D_MODEL, D_MODEL], BF16, kind="Internal").ap()
        WB1 = nc.dram_tensor("WB1", [DEPTH, D_MODEL, D_FF], BF16, kind="Internal").ap()
        WB2 = nc.dram_tensor("WB2", [DEPTH, D_FF, D_MODEL], BF16, kind="Internal").ap()
        XS = nc.dram_tensor("XS", [n_slots, SEQ, D_MODEL], F32, kind="Internal").ap()
        X1 = dscr("X1", (SEQ, D_MODEL), F32)
        QT = dscr("QT", (2048, SEQ), BF16)
        KT = dscr("KT", (1792, SEQ), BF16)
        VV = dscr("VV", (SEQ, 1792), BF16)
        MT = dscr("MT", (2048, SEQ), BF16)
        TP = nc.dram_tensor("TP", [DEPTH * 6 * 15, 127], F32, kind="Internal").ap()

        ARENA_B = 203008
        arena_t = st.enter_context(nc.sbuf_tensor("arena", [128, ARENA_B // 4], F32))
        cst_f = st.enter_context(nc.sbuf_tensor("cst_f", [128, CF_N], F32))
        cst_b = st.enter_context(nc.sbuf_tensor("cst_b", [128, CF_N], BF16))
        ps_t = [st.enter_context(nc.psum_tensor("ps%d" % i, [128, 512], F32)) for i in range(8)]
        PS = [t[:] for t in ps_t]
        PSB = [Buf(excl=True) for _ in range(8)]
        s = S(nc, st)
        A = Arena(arena_t[:], ARENA_B)
        CB = Buf()

        ident = cst_b[:, CF_IDENT:CF_IDENT + 128]
        ones_b = cst_b[:, CF_ONES:CF_ONES + 128]
        onesm_b = cst_b[:, CF_ONESM:CF_ONESM + 128]
        rta_b = cst_b[:, CF_RTA:CF_RTA + 128]
        rtb_b = cst_b[:, CF_RTB:CF_RTB + 128]
        jmat = cst_f[0:64, CF_J:CF_J + 64]
        cmask = cst_f[:, CF_CMASK:CF_CMASK + 64]
        bmask = [cst_b[:, CF_BM + 256 * i:CF_BM + 256 * (i + 1)] for i in range(3)]
        oc3 = cst_b[:, CF_OC3:CF_OC3 + 3]
        epsc = cst_f[:, CF_EPS:CF_EPS + 1]

        s.dma("sp", cst_f[:, :], cf_in[:, :], writes=[CB])
        s.op("dve", "tensor_copy", cst_b[:, :], cst_f[:, :], reads=[CB], writes=[CB])
        A.reset()
        zt = A.f32(381)
        zb = Buf()
        s.op("pool", "memset", zt[0:120, :], 0.0, writes=[zb])
        s.dma("sp", TP.rearrange("(p a) y -> p (a y)", a=3), zt[0:120, :], reads=[zb])
        s.barrier()
        s.dma("sp", TP[:, 48:79], crb.rearrange("l h d c -> (l h d) c"))
        s.barrier()

        def cast_weights(src, dst, rows, cols, cw):
            A.reset()
            nb = 3
            ib = [(A.f32(cw), Buf()) for _ in range(nb)]
            ob = [(A.bf16(cw), Buf()) for _ in range(nb)]
            jobs = []
            for l in layers:
                for rb in range(rows // 128):
                    for cb in range(cols // cw):
                        jobs.append((l, rb, cb))
            engs = ["act", "dve", "dve"]

            def load(i):
                l, rb, cb = jobs[i]
                s.dma("sp", ib[i % nb][0], src[l, rb * 128:(rb + 1) * 128, cb * cw:(cb + 1) * cw],
                      writes=[ib[i % nb][1]])
            for i in range(min(nb, len(jobs))):
                load(i)
            for i, (l, rb, cb) in enumerate(jobs):
                e = engs[i % 3]
                if e == "act":
                    s.op("act", "activation", ob[i % nb][0], ib[i % nb][0], AF.Copy,
                         reads=[ib[i % nb][1]], writes=[ob[i % nb][1]])
                else:
                    s.op(e, "tensor_copy", ob[i % nb][0], ib[i % nb][0],
                         reads=[ib[i % nb][1]], writes=[ob[i % nb][1]])
                s.dma("sp", dst[l, rb * 128:(rb + 1) * 128, cb * cw:(cb + 1) * cw], ob[i % nb][0],
                      reads=[ob[i % nb][1]])
                if i + nb < len(jobs):
                    load(i + nb)
            s.barrier()

        if "P" in phases and DBG["cast"]:
            cast_weights(w_in, WBi, D_MODEL, IN_W, 2816)
        if "O" in phases:
            cast_weights(w_out, WBo, D_MODEL, D_MODEL, 2048)
        if "F" in phases:
            cast_weights(w_ff1, WB1, D_MODEL, D_FF, 2048)
            cast_weights(w_ff2, WB2, D_FF, D_MODEL, 2048)

        class XLoader:
            def __init__(self, aux_banks, xin=None):
                self.xin = xin if xin is not None else [(A.f32(2048), Buf()) for _ in range(2)]
                one = [(A.bf16(2048), Buf()) for _ in range(4)]
                self.xbf = [one, one]
                self.aux = aux_banks
                self.cnt = 0
                self.lc = 0
                self.ec = 0

            def prefetch(self, src_rows_fn, gen):
                for stt in range(4):
                    xi, xib = self.xin[self.lc % 2]
                    self.lc += 1
                    s.dma("sp", xi, src_rows_fn(stt), writes=[xib])
                    xb, xbb = self.xbf[gen % 2][stt]
                    s.op("dve", "tensor_copy", xb, xi, reads=[xib], writes=[xbb])

            def transpose(self, gen, xT, xTb):
                for kp in range(8):
                    bk = self.aux[self.cnt % len(self.aux)]
                    self.cnt += 1
                    psb = PS[bk].bitcast(BF16)
                    for kk in range(2):
                        kc = 2 * kp + kk
                        for stt in range(4):
                            xb, xbb = self.xbf[gen % 2][stt]
                            last = (kk == 1 and stt == 3)
                            s.op("pe", "transpose", psb[:, kk * 512 + stt * 128:kk * 512 + (stt + 1) * 128],
                                 xb[:, kc * 128:(kc + 1) * 128], ident,
                                 reads=[xbb, CB], writes=[PSB[bk]], inc=last)
                    dst = xT[:, 2 * kp:2 * kp + 2, :].rearrange("p a n -> p (a n)")
                    if self.ec % 2 == 0:
                        s.op("act", "activation", dst, psb, AF.Copy, reads=[PSB[bk]], writes=[xTb])
                    else:
                        s.op("dve", "tensor_copy", dst, psb, reads=[PSB[bk]], writes=[xTb])
                    self.ec += 1

        def ln_epilogue(zb_ap, zb_buf, junk, junkb, stat, statb, g_ap, b_ap, gbuf, dst_rows):
            s.op("dve", "memset", stat[:, 8:16], 0.0, writes=[statb])
            for pc_ in range(4):
                zs = zb_ap[:, pc_ * 512:(pc_ + 1) * 512]
                s.op("act", "activation", junk, zs, AF.Identity, accum_out=stat[:, 8 + pc_:9 + pc_],
                     reads=[zb_buf, statb], writes=[junkb, statb])
                s.op("act", "activation", junk, zs, AF.Square, accum_out=stat[:, 12 + pc_:13 + pc_],
                     reads=[zb_buf, statb], writes=[junkb, statb])
            s.op("dve", "tensor_reduce", stat[:, 0:2], stat[:, 8:16].rearrange("p (a b) -> p a b", b=4),
                 mybir.AxisListType.X, ALU.add, reads=[statb], writes=[statb])
            s.op("dve", "tensor_scalar", stat[:, 2:4], stat[:, 0:2], 1.0 / D_MODEL, None, ALU.mult,
                 reads=[statb], writes=[statb])
            s.op("dve", "tensor_tensor", stat[:, 4:5], stat[:, 2:3], stat[:, 2:3], ALU.mult,
                 reads=[statb], writes=[statb])
            s.op("dve", "tensor_tensor", stat[:, 5:6], stat[:, 3:4], stat[:, 4:5], ALU.subtract,
                 reads=[statb], writes=[statb])
            s.op("act", "activation", stat[:, 6:7], stat[:, 5:6], AF.Sqrt, bias=epsc,
                 reads=[statb, CB], writes=[statb])
            s.op("dve", "reciprocal", stat[:, 6:7], stat[:, 6:7], reads=[statb], writes=[statb])
            s.op("dve", "scalar_tensor_tensor", stat[:, 7:8], stat[:, 2:3], -1.0, stat[:, 6:7],
                 ALU.mult, ALU.mult, reads=[statb], writes=[statb])
            s.op("act", "activation", zb_ap, zb_ap, AF.Identity, bias=stat[:, 7:8], scale=stat[:, 6:7],
                 reads=[zb_buf, statb], writes=[zb_buf])
            s.op("pool", "tensor_tensor", zb_ap, zb_ap, g_ap, ALU.mult, reads=[zb_buf, gbuf], writes=[zb_buf])
            s.op("pool", "tensor_tensor", zb_ap, zb_ap, b_ap, ALU.add, reads=[zb_buf, gbuf], writes=[zb_buf])
            s.dma("pool", dst_rows, zb_ap, reads=[zb_buf])

        def bcast_row(t, l):
            return dap(t, l * D_MODEL, [[0, 128], [1, D_MODEL]])

        def col_vec(t, l, n=128):
            return dap(t, l * n, [[1, n], [1, 1]])

        def phase_P(xsrc, l):
            A.reset()
            xl = XLoader([4, 5, 6, 7])
            xT = [(A.bf16(16 * 512).rearrange("p (k n) -> p k n", n=512), Buf()) for _ in range(2)]
            wb = [(A.bf16(16 * 512).rearrange("p (k n) -> p k n", n=512), Buf()) for _ in range(3)]
            rp = [(A.f32(4 * 512).rearrange("p (k n) -> p k n", n=512), Buf()) for _ in range(2)]
            gq = A.f32(2)
            gb = Buf()
            s.dma("sp", gq[:, 0:1], col_vec(aqg, l), writes=[gb])
            s.dma("sp", gq[:, 1:2], col_vec(akg, l), writes=[gb])
            qbf = [(A.bf16(512), Buf()) for _ in range(2)]
            sqb = [(A.bf16(512), Buf()) for _ in range(2)]
            t1 = [(A.f32(512), Buf()) for _ in range(2)]
            t2 = [(A.f32(512), Buf()) for _ in range(2)]
            rs = [(A.f32(512), Buf()) for _ in range(2)]
            og = [(A.bf16(512), Buf()) for _ in range(4)]
            cnt = {"main": 0, "aux": 0, "w": 0, "e": 0, "og": 0}

            def auxbank():
                b = 4 + cnt["aux"] % 4
                cnt["aux"] += 1
                return b
            pend = [None]

            def flush():
                if pend[0] is not None:
                    f_ = pend[0]
                    pend[0] = None
                    f_()

            def srcfn(tt):
                return lambda stt: xsrc[tt * 512 + stt * 128: tt * 512 + (stt + 1) * 128, :]

            NT = DBG["nt"]
            xl.prefetch(srcfn(0), 0)
            for tt in range(NT):
                xTa, xTb = xT[tt % 2]
                xl.transpose(tt, xTa, xTb)
                if tt + 1 < NT:
                    xl.prefetch(srcfn(tt + 1), tt + 1)
                rpa, rpb = rp[tt % 2]
                s.dma("sp", rpa, rope_in[:, :, tt * 512:(tt + 1) * 512].rearrange("k p n -> p k n"), writes=[rpb])
                for g in range(11):
                    wa, wbb = wb[cnt["w"] % 3]
                    cnt["w"] += 1
                    s.dma("sp", wa, WBi[l, :, g * 512:(g + 1) * 512].rearrange("(k p) n -> p k n", p=128),
                          writes=[wbb])
                    j = 0
                    while j < 4:
                        c = g * 4 + j
                        typ = CH_TYPES[c]
                        kind, di = CH_DEST[c]
                        if (typ[0] if kind != "v" else "V") not in DBG["ptypes"]:
                            j += 1
                            continue
                        if kind == "v":
                            j2 = j
                            while j2 < 4 and CH_DEST[g * 4 + j2][0] == "v":
                                j2 += 1
                            ncol = (j2 - j) * 128
                            for stt in range(4):
                                bk = cnt["main"] % 4
                                cnt["main"] += 1
                                for kc in range(16):
                                    s.op("pe", "matmul", PS[bk][:, 0:ncol], xTa[:, kc, stt * 128:(stt + 1) * 128],
                                         wa[:, kc, j * 128:j2 * 128], start=(kc == 0), stop=(kc == 15),
                                         reads=[xTb, wbb], writes=[PSB[bk]], inc=(kc == 15))
                                oa, ob = og[cnt["og"] % 4]
                                cnt["og"] += 1
                                if cnt["e"] % 2 == 0:
                                    s.op("act", "activation", oa[:, 0:ncol], PS[bk][:, 0:ncol], AF.Copy,
                                         reads=[PSB[bk]], writes=[ob])
                                else:
                                    s.op("dve", "tensor_copy", oa[:, 0:ncol], PS[bk][:, 0:ncol],
                                         reads=[PSB[bk]], writes=[ob])
                                cnt["e"] += 1
                                r0 = tt * 512 + stt * 128
                                s.dma("pool", VV[r0:r0 + 128, di * 128:di * 128 + ncol], oa[:, 0:ncol], reads=[ob])
                                flush()
                            j = j2
                            continue
                        bk = cnt["main"] % 4
                        cnt["main"] += 1
                        for kc in range(16):
                            s.op("pe", "matmul", PS[bk], wa[:, kc, j * 128:(j + 1) * 128], xTa[:, kc, :],
                                 start=(kc == 0), stop=(kc == 15), reads=[xTb, wbb], writes=[PSB[bk]],
                                 inc=(kc == 15))
                        oa, ob = og[cnt["og"] % 4]
                        cnt["og"] += 1
                        dst = (QT if kind == "q" else KT)[di * 128:(di + 1) * 128, tt * 512:(tt + 1) * 512]
                        if typ[0] == "C":
                            s.op("act", "activation", oa, PS[bk], AF.Copy, reads=[PSB[bk]], writes=[ob])
                        elif typ[0] == "B":
                            i2 = cnt["e"] % 2
                            cnt["e"] += 1
                            qa, qb = qbf[i2]
                            s.op("act", "activation", qa, PS[bk], AF.Copy, reads=[PSB[bk]], writes=[qb])

                            def part2(bk=bk, qa=qa, qb=qb, i2=i2, oa=oa, ob=ob, dst=dst, rpa=rpa, rpb=rpb):
                                b2 = auxbank()
                                s.op("pe", "matmul", PS[b2], rtb_b, qa, start=True, stop=True,
                                     reads=[qb, CB], writes=[PSB[b2]])
                                ta, tb = t1[i2]
                                ua, ub = t2[i2]
                                s.op("dve", "tensor_tensor", ta, PS[bk], rpa[:, 2, :], ALU.mult,
                                     reads=[PSB[bk], rpb], writes=[tb])
                                s.op("dve", "tensor_tensor", ua, PS[b2], rpa[:, 3, :], ALU.mult,
                                     reads=[PSB[b2], rpb], writes=[ub])
                                s.op("pool", "tensor_tensor", oa, ta, ua, ALU.add, reads=[tb, ub], writes=[ob])
                                s.dma("pool", dst, oa, reads=[ob])
                            flush()
                            pend[0] = part2
                            j += 1
                            continue
                        else:
                            i2 = cnt["e"] % 2
                            cnt["e"] += 1
                            gcol = gq[:, 0:1] if kind == "q" else gq[:, 1:2]
                            qa, qb = qbf[i2]
                            sa, sb = sqb[i2]
                            s.op("act", "activation", qa, PS[bk], AF.Copy, scale=gcol,
                                 reads=[PSB[bk], gb], writes=[qb])
                            s.op("act", "activation", sa, PS[bk], AF.Square, reads=[PSB[bk]], writes=[sb])

                            def part2(bk=bk, qa=qa, qb=qb, sa=sa, sb=sb, i2=i2, oa=oa, ob=ob, dst=dst, rpa=rpa,
                                      rpb=rpb, gcol=gcol):
                                b2 = auxbank()
                                s.op("pe", "matmul", PS[b2], rta_b, qa, start=True, stop=True,
                                     reads=[qb, CB], writes=[PSB[b2]])
                                b3 = auxbank()
                                s.op("pe", "matmul", PS[b3], onesm_b, sa, start=True, stop=True,
                                     reads=[sb, CB], writes=[PSB[b3]])
                                ta, tb = t1[i2]
                                ua, ub = t2[i2]
                                ra_, rb_ = rs[i2]
                                s.op("dve", "scalar_tensor_tensor", ta, PS[bk], gcol, rpa[:, 0, :], ALU.mult,
                                     ALU.mult, reads=[PSB[bk], rpb, gb], writes=[tb])
                                s.op("dve", "tensor_tensor", ua, PS[b2], rpa[:, 1, :], ALU.mult,
                                     reads=[PSB[b2], rpb], writes=[ub])
                                s.op("act", "activation", ra_, PS[b3], AF.Sqrt, bias=epsc,
                                     reads=[PSB[b3], CB], writes=[rb_])
                                s.op("dve", "reciprocal", ra_, ra_, reads=[rb_], writes=[rb_])
                                s.op("pool", "tensor_tensor", ta, ta, ua, ALU.add, reads=[tb, ub], writes=[tb])
                                s.op("pool", "tensor_tensor", oa, ta, ra_, ALU.mult, reads=[tb, rb_], writes=[ob])
                                s.dma("pool", dst, oa, reads=[ob])
                            flush()
                            pend[0] = part2
                            j += 1
                            continue
                        flush()
                        s.dma("pool", dst, oa, reads=[ob])
                        j += 1
                flush()
            s.barrier()

        def finalize_on(o_bk, l_bk, rl, rlb, oa, ob, dst):
            s.op("dve", "reciprocal", rl, PS[l_bk], reads=[PSB[l_bk]], writes=[rlb])
            s.op("dve", "tensor_tensor", oa, PS[o_bk], rl, ALU.mult, reads=[PSB[o_bk], rlb], writes=[ob])
            s.dma("pool", dst, oa, reads=[ob])

        def phase_A():
            A.reset()
            qa = A.bf16(4 * SEQ).rearrange("p (h t) -> p h t", t=SEQ)
            ka = A.bf16(2 * SEQ).rearrange("p (h t) -> p h t", t=SEQ)
            va = A.bf16(32 * 256).rearrange("p (k c) -> p k c", c=256)
            ib = Buf()
            for h in range(4):
                s.dma("sp", qa[:, h, :], QT[h * 128:(h + 1) * 128, :], writes=[ib])
            for h in range(2):
                s.dma("sp", ka[:, h, :], KT[h * 128:(h + 1) * 128, :], writes=[ib])
            s.dma("sp", va, VV[:, 0:256].rearrange("(k p) c -> p k c", p=128), writes=[ib])
            E = [(A.bf16(512), Buf()) for _ in range(4)]
            rl = [(A.f32(512), Buf()) for _ in range(2)]
            og = [(A.bf16(512), Buf()) for _ in range(2)]
            cnt = 0
            idx = 0
            for h in range(4):
                g = h // 2
                for qt in range(8):
                    o_bk = 4 + idx % 2
                    l_bk = 6 + idx % 2

                    def S_mm(kc, c):
                        bk = c % 4
                        s.op("pe", "matmul", PS[bk], ka[:, g, kc * 128:(kc + 1) * 128],
                             qa[:, h, qt * 512:(qt + 1) * 512], start=True, stop=True,
                             reads=[ib], writes=[PSB[bk]])
                    S_mm(0, cnt)
                    S_mm(1, cnt + 1)
                    for kc in range(32):
                        c = cnt + kc
                        if kc + 2 < 32:
                            S_mm(kc + 2, c + 2)
                        ea, eb = E[c % 4]
                        s.op("act", "activation", ea, PS[c % 4], AF.Exp, scale=SCALE,
                             reads=[PSB[c % 4]], writes=[eb])
                        s.op("pe", "matmul", PS[o_bk], va[:, kc, g * 128:(g + 1) * 128], ea,
                             start=(kc == 0), stop=(kc == 31), reads=[eb, ib], writes=[PSB[o_bk]], inc=False)
                        s.op("pe", "matmul", PS[l_bk], ones_b, ea, start=(kc == 0), stop=(kc == 31),
                             reads=[eb, CB], writes=[PSB[l_bk]], inc=True)
                    cnt += 32
                    finalize_on(o_bk, l_bk, rl[idx % 2][0], rl[idx % 2][1], og[idx % 2][0], og[idx % 2][1],
                                MT[h * 128:(h + 1) * 128, qt * 512:(qt + 1) * 512])
                    idx += 1
            s.barrier()

        def phase_B2():
            for hp in range(2):
                A.reset()
                U = [(A.f32(SEQ), Buf()) for _ in range(3)]
                ls, lsb = A.f32(SEQ), Buf()
                raw = [(A.bf16(SEQ), Buf()) for _ in range(2)]
                E = [(A.bf16(256), Buf()) for _ in range(6)]
                rl = [(A.f32(512), Buf()) for _ in range(2)]
                og = [(A.bf16(512), Buf()) for _ in range(3)]
                cnt = 0
                ev = 0
                for g in range(3):
                    D = B_DIL[g]
                    L = SEQ // D
                    nb = L // 128
                    hb = 2 * g + hp
                    qs = A.bf16(SEQ).rearrange("p (r i) -> p r i", r=D)
                    ks = A.bf16(D * (L + 128)).rearrange("p (r i) -> p r i", r=D)
                    vs = A.bf16(D * (nb + 1) * 128).rearrange("p (r c d) -> p r c d", r=D, d=128)
                    qsb, ksb, vsb = Buf(), Buf(), Buf()
                    ra, rb = raw[0]
                    s.dma("sp", ra, QT[(4 + hb) * 128:(5 + hb) * 128, :], writes=[rb])
                    s.op("dve", "tensor_copy", qs, ra.rearrange("p (i r) -> p r i", r=D), reads=[rb], writes=[qsb])
                    ra, rb = raw[1]
                    s.dma("sp", ra, KT[(2 + hb) * 128:(3 + hb) * 128, :], writes=[rb])
                    s.op("pool", "memset", ks[:, :, 0:64], 0.0, writes=[ksb])
                    s.op("pool", "memset", ks[:, :, L + 64:L + 128], 0.0, writes=[ksb])
                    s.op("dve", "tensor_copy", ks[:, :, 64:64 + L], ra.rearrange("p (i r) -> p r i", r=D),
                         reads=[rb], writes=[ksb])
                    s.op("pool", "memset", vs[0:64, :, 0, :], 0.0, writes=[vsb])
                    s.op("pool", "memset", vs[64:128, :, nb, :], 0.0, writes=[vsb])
                    col0 = 256 + hb * 128
                    for r in range(D):
                        s.dma("sp", vs[64:128, r, 0:nb, :],
                              dap(VV, r * 1792 + col0, [[D * 1792, 64], [128 * D * 1792, nb], [1, 128]]),
                              writes=[vsb])
                        s.dma("sp", vs[0:64, r, 1:nb + 1, :],
                              dap(VV, (64 * D + r) * 1792 + col0, [[D * 1792, 64], [128 * D * 1792, nb], [1, 128]]),
                              writes=[vsb])
                    Ua, Ub = U[g]
                    items = [(r, c) for r in range(D) for c in range(nb + 1)]
                    base = cnt

                    def geom(c):
                        a = 0 if c >= 1 else 128
                        b = 256 if c <= nb - 1 else 128
                        return a, b

                    def S_mm(i):
                        r, c = items[i]
                        a, b = geom(c)
                        qlo = 128 * (c - 1) + a
                        qhi = 128 * (c - 1) + b
                        mk = bmask[1] if c == 0 else (bmask[2] if c == nb else bmask[0])
                        bk = (base + i) % 4
                        s.op("pe", "matmul", PS[bk][:, a:b], ks[:, r, 128 * c:128 * c + 128], qs[:, r, qlo:qhi],
                             start=True, stop=False, reads=[ksb, qsb], writes=[PSB[bk]], inc=False)
                        s.op("pe", "matmul", PS[bk][:, a:b], ident, mk[:, a:b], start=False, stop=True,
                             reads=[CB], writes=[PSB[bk]])
                    S_mm(0)
                    S_mm(1)
                    slot = 0
                    bstart = 0
                    o_bk = l_bk = None
                    for i, (r, c) in enumerate(items):
                        if i + 2 < len(items):
                            S_mm(i + 2)
                        a, b = geom(c)
                        bk = (base + i) % 4
                        ea, eb = E[(base + i) % 6]
                        s.op("act", "activation", ea[:, a:b], PS[bk][:, a:b], AF.Exp, scale=SCALE,
                             reads=[PSB[bk]], writes=[eb])
                        if c >= 1:
                            blk = c - 1
                            if slot == 0:
                                o_bk = 4 + ev % 2
                                l_bk = 6 + ev % 2
                                ev += 1
                                bstart = blk
                            pa, pb = E[(base + i - 1) % 6]
                            cs = slice(slot * 128, (slot + 1) * 128)
                            s.op("pe", "matmul", PS[o_bk][:, cs], vs[:, r, c - 1, :], pa[:, 128:256],
                                 start=True, stop=False, reads=[vsb, pb], writes=[PSB[o_bk]], inc=False)
                            s.op("pe", "matmul", PS[o_bk][:, cs], vs[:, r, c, :], ea[:, 0:128],
                                 start=False, stop=True, reads=[vsb, eb], writes=[PSB[o_bk]], inc=False)
                            s.op("pe", "matmul", PS[l_bk][:, cs], ones_b, pa[:, 128:256],
                                 start=True, stop=False, reads=[CB, pb], writes=[PSB[l_bk]], inc=False)
                            s.op("pe", "matmul", PS[l_bk][:, cs], ones_b, ea[:, 0:128],
                                 start=False, stop=True, reads=[CB, eb], writes=[PSB[l_bk]], inc=True)
                            slot += 1
                            if slot == 4 or blk == nb - 1:
                                n = slot
                                t0 = r + 128 * bstart * D
                                t1_ = t0 + (n * 128 - 1) * D + 1
                                s.op("dve", "tensor_copy", Ua[:, t0:t1_:D], PS[o_bk][:, 0:n * 128],
                                     reads=[PSB[o_bk]], writes=[Ub])
                                if g == 0:
                                    s.op("dve", "tensor_copy", ls[:, t0:t1_:D], PS[l_bk][:, 0:n * 128],
                                         reads=[PSB[l_bk]], writes=[lsb])
                                else:
                                    s.op("dve", "tensor_tensor", ls[:, t0:t1_:D], PS[l_bk][:, 0:n * 128],
                                         ls[:, t0:t1_:D], ALU.add, reads=[PSB[l_bk], lsb], writes=[lsb])
                                slot = 0
                    cnt += len(items)
                for qt in range(8):
                    ra_, rb_ = rl[qt % 2]
                    s.op("dve", "reciprocal", ra_, ls[:, qt * 512:(qt + 1) * 512], reads=[lsb], writes=[rb_])
                    for g in range(3):
                        oa, ob = og[(qt * 3 + g) % 3]
                        hb = 2 * g + hp
                        s.op("pool", "tensor_tensor", oa, U[g][0][:, qt * 512:(qt + 1) * 512], ra_, ALU.mult,
                             reads=[U[g][1], rb_], writes=[ob])
                        s.dma("pool", MT[(4 + hb) * 128:(5 + hb) * 128, qt * 512:(qt + 1) * 512], oa, reads=[ob])
                s.barrier()

        def r0_of(r):
            return min(max(r - 4, 0), 56)

        def phase_C(l):
            A.reset()
            sets = []
            for i in range(2):
                sets.append(dict(
                    qa=A.bf16(SEQ), ka=A.bf16(SEQ),
                    va=A.bf16(32 * 128).rearrange("p (k c) -> p k c", c=128), ib=Buf(),
                    H2=A.f32(15 * 128).rearrange("p (d x) -> p d x", x=128), hb=Buf(),
                    CmR=A.f32(15 * 64).rearrange("p (e q) -> p e q", q=64), cb=Buf(),
                    tiles=[(A.bf16(512), Buf()) for _ in range(20)]))
            E = [(A.bf16(512), Buf()) for _ in range(4)]
            rl = [(A.f32(512), Buf()) for _ in range(2)]
            og = [(A.bf16(512), Buf()) for _ in range(2)]
            state = {"cnt": 0, "mi": 0}

            def prep(h):
                st_ = sets[h % 2]
                qa, ka, va, ib = st_["qa"], st_["ka"], st_["va"], st_["ib"]
                s.dma("sp", qa, QT[(10 + h) * 128:(11 + h) * 128, :], writes=[ib])
                s.dma("sp", ka, KT[(8 + h) * 128:(9 + h) * 128, :], writes=[ib])
                s.dma("sp", va, VV[:, 1024 + h * 128:1024 + (h + 1) * 128].rearrange("(k p) c -> p k c", p=128),
                      writes=[ib])
                H2, hb_, CmR, cb_ = st_["H2"], st_["hb"], st_["CmR"], st_["cb"]
                base = ((l * 6 + h) * 15) * 127
                for half in range(2):
                    s.dma("sp", H2[0:64, :, half * 64:(half + 1) * 64],
                          dap(TP, base, [[1, 64], [127, 15], [1, 64]]), writes=[hb_])
                for dr in range(15):
                    bk = 2 + (dr // 8)
                    sl = slice((dr % 8) * 64, (dr % 8 + 1) * 64)
                    s.op("pe", "matmul", PS[bk][:, sl], H2[0:64, dr, :], jmat, start=True, stop=True,
                         reads=[hb_, CB], writes=[PSB[bk]])
                    s.op("dve", "scalar_tensor_tensor", CmR[:, 14 - dr, :], PS[bk][:, sl], 1.0 / SCALE, cmask,
                         ALU.mult, ALU.add, reads=[PSB[bk], CB], writes=[cb_])
                sigs = {}
                plan = []
                for m in range(8):
                    rows = [8 * m + i for i in range(8)]
                    lo = min(r0_of(r) for r in rows)
                    hi = max(r0_of(r) + 7 for r in rows)
                    js = list(range(lo // 2, hi // 2 + 1))
                    for n, j in enumerate(js):
                        sig = []
                        for arel in range(2):
                            a_ = 2 * j + arel
                            val = [i for i, r in enumerate(rows) if r0_of(r) <= a_ <= r0_of(r) + 7]
                            if val:
                                rlo, rhi = val[0], val[-1] + 1
                                assert val == list(range(rlo, rhi))
                                elo = 7 - a_ + rows[rlo]
                                assert 0 <= elo and elo + (rhi - rlo) <= 15
                                sig.append((rlo, rhi, elo))
                            else:
                                sig.append(None)
                        sig = tuple(sig)
                        if sig not in sigs:
                            ti = len(sigs)
                            ta, tb = st_["tiles"][ti]
                            s.op("pool", "memset", ta, NEGM, writes=[tb])
                            for arel in range(2):
                                if sig[arel] is None:
                                    continue
                                rlo, rhi, elo = sig[arel]
                                pr = slice(arel * 64, (arel + 1) * 64)
                                s.op("pool", "tensor_copy",
                                     ta[pr, rlo * 64:rhi * 64].rearrange("p (n q) -> p n q", q=64),
                                     CmR[pr, elo:elo + (rhi - rlo), :], reads=[cb_], writes=[tb])
                            sigs[sig] = ti
                        plan.append((m, n, len(js), j, sigs[sig]))
                return plan

            def main(h, plan):
                st_ = sets[h % 2]
                qa, ka, va, ib = st_["qa"], st_["ka"], st_["va"], st_["ib"]
                base = state["cnt"]

                def S_mm(i):
                    m, n, nn, j, ti = plan[i]
                    bk = (base + i) % 4
                    ta, tb = st_["tiles"][ti]
                    s.op("pe", "matmul", PS[bk], ka[:, j * 128:(j + 1) * 128], qa[:, m * 512:(m + 1) * 512],
                         start=True, stop=False, reads=[ib], writes=[PSB[bk]], inc=False)
                    s.op("pe", "matmul", PS[bk], ident, ta, start=False, stop=True,
                         reads=[CB, tb], writes=[PSB[bk]])
                S_mm(0)
                S_mm(1)
                for i, (m, n, nn, j, ti) in enumerate(plan):
                    c = base + i
                    if i + 2 < len(plan):
                        S_mm(i + 2)
                    if n == 0:
                        state["mi"] += 1
                    o_bk = 4 + state["mi"] % 2
                    l_bk = 6 + state["mi"] % 2
                    ea, eb = E[c % 4]
                    s.op("act", "activation", ea, PS[c % 4], AF.Exp, scale=SCALE, reads=[PSB[c % 4]], writes=[eb])
                    s.op("pe", "matmul", PS[o_bk], va[:, j, :], ea, start=(n == 0), stop=(n == nn - 1),
                         reads=[ib, eb], writes=[PSB[o_bk]], inc=False)
                    s.op("pe", "matmul", PS[l_bk], ones_b, ea, start=(n == 0), stop=(n == nn - 1),
                         reads=[CB, eb], writes=[PSB[l_bk]], inc=True)
                    if n == nn - 1:
                        k2 = state["mi"] % 2
                        finalize_on(o_bk, l_bk, rl[k2][0], rl[k2][1], og[k2][0], og[k2][1],
                                    MT[(10 + h) * 128:(11 + h) * 128, m * 512:(m + 1) * 512])
                state["cnt"] += len(plan)

            plans = {0: prep(0)}
            for h in range(6):
                if h + 1 < 6:
                    plans[h + 1] = prep(h + 1)
                main(h, plans[h])
            s.barrier()

        GRP = [(0, 4), (4, 10), (10, 16)]

        def phase_O(xsrc, l):
            A.reset()
            W = A.bf16(16 * 2048).rearrange("p (k n) -> p k n", n=2048)
            Wb = Buf()
            for q4 in range(4):
                s.dma("sp", W[:, 4 * q4:4 * q4 + 4, :],
                      WBo[l, 512 * q4:512 * (q4 + 1), :].rearrange("(k p) n -> p k n", p=128), writes=[Wb])
            mg = A.f32(16)
            mgb = Buf()
            s.dma("sp", mg, dap(mixg, l * D_MODEL, [[1, 128], [128, 16]]), writes=[mgb])
            for kc in range(16):
                s.op("dve", "tensor_scalar", W[:, kc, :], W[:, kc, :], mg[:, kc:kc + 1], None,
                     ALU.mult, reads=[Wb, mgb], writes=[Wb])
            gt = A.f32(2048)
            bt = A.f32(2048)
            gbuf = Buf()
            s.dma("sp", gt, bcast_row(ln1g, l), writes=[gbuf])
            s.dma("sp", bt, bcast_row(ln1b, l), writes=[gbuf])
            mt = [(A.bf16(16 * 512).rearrange("p (k n) -> p k n", n=512), Buf()) for _ in range(2)]
            sq = (A.bf16(16 * 512).rearrange("p (k n) -> p k n", n=512), Buf())
            xb = [(A.f32(2048), Buf()) for _ in range(5)]
            junk, junkb = A.bf16(512), Buf()
            stat = [(A.f32(16), Buf()) for _ in range(5)]
            r3 = [(A.f32(4), Buf()) for _ in range(5)]
            NT = SEQ // 512
            cnt = 0
            bkc = 0
            pend_ln = None
            for tt in range(NT):
                ma, mb = mt[tt % 2]
                s.dma("sp", ma, MT[:, tt * 512:(tt + 1) * 512].rearrange("(k p) n -> p k n", p=128), writes=[mb])
                s.op("act", "activation", sq[0].rearrange("p k n -> p (k n)"), ma.rearrange("p k n -> p (k n)"),
                     AF.Square, reads=[mb], writes=[sq[1]])
                for stt in range(4):
                    xa, xbb = xb[cnt % 5]
                    sa, sb = stat[cnt % 5]
                    ra_, rb_ = r3[cnt % 5]
                    cnt += 1
                    r0 = tt * 512 + stt * 128
                    s.dma("sp", xa, xsrc[r0:r0 + 128, :], writes=[xbb])
                    s.op("act", "activation", xa, xa, AF.Copy, scale=ALPHA, reads=[xbb], writes=[xbb])
                    for gi, (k0, k1) in enumerate(GRP):
                        for kc in range(k0, k1):
                            s.op("pe", "matmul", PS[7][:, gi:gi + 1], sq[0][:, kc, stt * 128:(stt + 1) * 128],
                                 oc3[:, gi:gi + 1], start=(kc == k0), stop=(kc == k1 - 1),
                                 reads=[sq[1], CB], writes=[PSB[7]], inc=(kc == k1 - 1))
                    s.op("act", "activation", ra_[:, 0:3], PS[7][:, 0:3], AF.Sqrt, bias=epsc,
                         reads=[PSB[7], CB], writes=[rb_])
                    s.op("dve", "reciprocal", ra_[:, 0:3], ra_[:, 0:3], reads=[rb_], writes=[rb_])
                    for cg in range(4):
                        cs = slice(cg * 512, (cg + 1) * 512)
                        bks = []
                        for gi, (k0, k1) in enumerate(GRP):
                            bk = bkc % 6
                            bkc += 1
                            bks.append(bk)
                            for kc in range(k0, k1):
                                s.op("pe", "matmul", PS[bk], ma[:, kc, stt * 128:(stt + 1) * 128], W[:, kc, cs],
                                     start=(kc == k0), stop=(kc == k1 - 1), reads=[mb, Wb], writes=[PSB[bk]],
                                     inc=(kc == k1 - 1))
                        for gi in range(3):
                            s.op("dve", "scalar_tensor_tensor", xa[:, cs], PS[bks[gi]], ra_[:, gi:gi + 1], xa[:, cs],
                                 ALU.mult, ALU.add, reads=[PSB[bks[gi]], rb_, xbb], writes=[xbb])
                    if pend_ln is not None:
                        ln_epilogue(*pend_ln)
                    pend_ln = (xa, xbb, junk, junkb, sa, sb, gt, bt, gbuf, X1[r0:r0 + 128, :])
            if pend_ln is not None:
                ln_epilogue(*pend_ln)
            s.barrier()

        def phase_F(xdst, l):
            A.reset()
            xb = [(A.f32(2048), Buf()) for _ in range(4)]
            xl = XLoader([0, 1, 2, 3, 4, 5, 6, 7], xin=xb[0:2])
            xT = (A.bf16(16 * 512).rearrange("p (k n) -> p k n", n=512), Buf())
            hT = (A.bf16(64 * 512).rearrange("p (k n) -> p k n", n=512), Buf())
            wb = [(A.bf16(16 * 512).rearrange("p (k n) -> p k n", n=512), Buf()) for _ in range(3)]
            rt_ = [(A.f32(512), Buf()) for _ in range(2)]
            gt = A.f32(2048)
            bt = A.f32(2048)
            gbuf = Buf()
            s.dma("sp", gt, bcast_row(ln2g, l), writes=[gbuf])
            s.dma("sp", bt, bcast_row(ln2b, l), writes=[gbuf])
            junk, junkb = A.bf16(512), Buf()
            stat = [(A.f32(16), Buf()) for _ in range(4)]
            wc = 0
            pc = 0
            rc = 0

            def srcfn(tt):
                return lambda stt: X1[tt * 512 + stt * 128: tt * 512 + (stt + 1) * 128, :]
            NT = SEQ // 512
            pend_f = []
            xl.prefetch(srcfn(0), 0)
            for tt in range(NT):
                xl.transpose(tt, xT[0], xT[1])
                for fg in range(16):
                    if 1 <= fg <= 4 and pend_f:
                        ln_epilogue(*pend_f.pop(0))
                    if fg == 6 and tt + 1 < NT:
                        xl.prefetch(srcfn(tt + 1), tt + 1)
                    wa, wbb = wb[wc % 3]
                    wc += 1
                    s.dma("sp", wa, WB1[l, :, fg * 512:(fg + 1) * 512].rearrange("(k p) n -> p k n", p=128),
                          writes=[wbb])
                    for j in range(4):
                        fc = fg * 4 + j
                        bk = pc % 8
                        pc += 1
                        for kc in range(16):
                            s.op("pe", "matmul", PS[bk], wa[:, kc, j * 128:(j + 1) * 128], xT[0][:, kc, :],
                                 start=(kc == 0), stop=(kc == 15), reads=[wbb, xT[1]], writes=[PSB[bk]],
                                 inc=(kc == 15))
                        ra_, rb_ = rt_[rc % 2]
                        rc += 1
                        s.op("act", "activation", ra_, PS[bk], AF.Relu, reads=[PSB[bk]], writes=[rb_])
                        s.op("dve", "tensor_tensor", hT[0][:, fc, :], ra_, ra_, ALU.mult, reads=[rb_], writes=[hT[1]])
                def reload_residual():
                    for stt in range(4):
                        xa, xbb = xb[stt]
                        r0 = tt * 512 + stt * 128
                        s.dma("sp", xa, X1[r0:r0 + 128, :], writes=[xbb])
                        s.op("act", "activation", xa, xa, AF.Copy, scale=ALPHA, reads=[xbb], writes=[xbb])
                for cg in range(4):
                    cs = slice(cg * 512, (cg + 1) * 512)
                    bset = [(cg % 2) * 4 + i for i in range(4)]
                    for piece in range(4):
                        wa, wbb = wb[wc % 3]
                        wc += 1
                        s.dma("sp", wa, WB2[l, piece * 2048:(piece + 1) * 2048, cs].rearrange("(k p) n -> p k n", p=128),
                              writes=[wbb])
                        for stt in range(4):
                            for kc in range(16):
                                fc = piece * 16 + kc
                                s.op("pe", "matmul", PS[bset[stt]], hT[0][:, fc, stt * 128:(stt + 1) * 128],
                                     wa[:, kc, :], start=(fc == 0), stop=(fc == 63), reads=[hT[1], wbb],
                                     writes=[PSB[bset[stt]]], inc=(kc == 15))
                        if cg == 0 and piece == 1:
                            reload_residual()
                    for stt in range(4):
                        xa, xbb = xb[stt]
                        s.op("dve", "tensor_tensor", xa[:, cs], PS[bset[stt]], xa[:, cs], ALU.add,
                             reads=[PSB[bset[stt]], xbb], writes=[xbb])
                pc = 0
                for stt in range(4):
                    xa, xbb = xb[stt]
                    r0 = tt * 512 + stt * 128
                    pend_f.append((xa, xbb, junk, junkb, stat[stt][0], stat[stt][1], gt, bt, gbuf,
                                   xdst[r0:r0 + 128, :]))
            while pend_f:
                ln_epilogue(*pend_f.pop(0))
            s.barrier()

        nl = len(layers)
        for slot in range(n_slots):
            for li, l in enumerate(layers):
                xsrc = x_in[slot] if li == 0 else XS[slot]
                xdst = y_out[slot] if li == nl - 1 else XS[slot]
                if "P" in phases:
                    phase_P(xsrc, l)
                if "A" in phases:
                    phase_A()
                if "B" in phases:
                    phase_B2()
                if "C" in phases:
                    phase_C(l)
                if "O" in phases:
                    phase_O(xsrc, l)
                if "F" in phases:
                    phase_F(xdst, l)
        s.barrier()

        with nc.allow_non_contiguous_dma("tiny per-layer parameter vectors"):
            with nc.Block() as block:
                @block.tensor
                def _(e):
                    s.replay("pe", e)

                @block.scalar
                def _(e):
                    s.replay("act", e)

                @block.vector
                def _(e):
                    s.replay("dve", e)

                @block.gpsimd
                def _(e):
                    s.replay("pool", e)

                @block.sync
                def _(e):
                    s.replay("sp", e)
    return nc


_NC_CACHE = {}


def kernel(x_prompt, x_sample, w_in, a_q_gain, a_k_gain, c_rel_bias, mix_gain, w_out,
           ln1_g, ln1_b, w_ff1, w_ff2, ln2_g, ln2_b):
    f = lambda a: np.ascontiguousarray(np.asarray(a, dtype=np.float32))
    xp, xs = f(x_prompt), f(x_sample)
    seqs = [xp[i] for i in range(4)] + [xs[i] for i in range(8)]
    slot_seq = []
    for c in range(8):
        if c < 4:
            slot_seq.append((c, 8 + c))
        else:
            slot_seq.append((c, c))
    cf, rope = host_consts()
    shared = dict(w_in=f(w_in), w_out=f(w_out), w_ff1=f(w_ff1), w_ff2=f(w_ff2),
                  a_q_gain=f(a_q_gain), a_k_gain=f(a_k_gain), c_rel_bias=f(c_rel_bias),
                  mix_gain=f(mix_gain), ln1_g=f(ln1_g), ln1_b=f(ln1_b), ln2_g=f(ln2_g), ln2_b=f(ln2_b),
                  cf=cf, rope=rope)
    in_maps = []
    for c in range(8):
        m = dict(shared)
        m["x"] = np.stack([seqs[slot_seq[c][0]], seqs[slot_seq[c][1]]])
        in_maps.append(m)
    if "nc" not in _NC_CACHE:
        _NC_CACHE["nc"] = build()
    res = run_bass_kernel_spmd(_NC_CACHE["nc"], in_maps, core_ids=list(range(8)))
    outs = [None] * 12
    for c in range(8):
        y = res.results[c]["y"]
        outs[slot_seq[c][0]] = y[0]
        if c < 4:
            outs[slot_seq[c][1]] = y[1]
    y_prompt = np.stack(outs[0:4]).astype(np.float32)
    y_sample = np.stack(outs[4:12]).astype(np.float32)
    return (y_prompt, y_sample)
```

```python
import math
from contextlib import ExitStack

import numpy as np
import concourse.bass as bass
import concourse.mybir as mybir
from concourse.bass_utils import run_bass_kernel_spmd

F32 = mybir.dt.float32
BF16 = mybir.dt.bfloat16
ALU = mybir.AluOpType
AF = mybir.ActivationFunctionType

D_MODEL = 2048
SEQ = 4096
DEPTH = 4
IN_W = 5632
D_FF = 8192
EPS = 1e-6
ALPHA = (2 * DEPTH) ** 0.25
SCALE = 128 ** -0.5
NEGM = -30000.0
NDSEM = 40
B_DIL = (1, 4, 16)
DBG = {"nt": 8, "cast": True, "ptypes": "ABCV"}

CH_TYPES = (["Aq"] * 4 + ["Ak"] * 2 + ["Av"] * 2 + ["Bq"] * 6 + ["Bk"] * 6 + ["Bv"] * 6
            + ["Cq"] * 6 + ["Ck"] * 6 + ["Cv"] * 6)
assert len(CH_TYPES) == 44


def _chunk_dest():
    q = k = v = 0
    out = []
    for t in CH_TYPES:
        if t[1] == "q":
            out.append(("q", q)); q += 1
        elif t[1] == "k":
            out.append(("k", k)); k += 1
        else:
            out.append(("v", v)); v += 1
    return out


CH_DEST = _chunk_dest()


class Buf:
    __slots__ = ("w", "r", "excl")

    def __init__(self, excl=False):
        self.w = None
        self.r = {}
        self.excl = excl


class S:
    def __init__(self, nc, st):
        self.nc = nc
        self.names = ["pe", "act", "dve", "pool", "sp"]
        self.sems = []
        self.val = []
        self.esem = {}
        for e in self.names:
            self.esem[e] = self._newsem(st, "e_" + e)
        self.dsem = [self._newsem(st, "d%d" % i) for i in range(NDSEM)]
        self.dq = {"sp": self.dsem[:24], "pool": self.dsem[24:]}
        self.drr = {"sp": 0, "pool": 0}
        self.waited = {e: {} for e in self.names}
        self.ops = {e: [] for e in self.names}
        self.pend_r = []
        self.pend_w = []

    def _newsem(self, st, name):
        h = st.enter_context(self.nc.semaphore(name))
        self.sems.append(h)
        self.val.append(0)
        return len(self.sems) - 1

    def _deps(self, eng, reads, writes, extra=()):
        deps = {}

        def add(s, v):
            if deps.get(s, 0) < v:
                deps[s] = v
        for b in reads:
            if b.w is not None:
                add(*b.w)
            if b.excl:
                for s, v in b.r.items():
                    if s != self.esem.get(eng):
                        add(s, v)
        for b in writes:
            if b.w is not None:
                add(*b.w)
            for s, v in b.r.items():
                add(s, v)
        for s, v in extra:
            add(s, v)
        w = self.waited[eng]
        for s, v in deps.items():
            if eng == "pe" and s == self.esem["pe"]:
                continue
            if w.get(s, 0) < v:
                self.ops[eng].append(("w", s, v))
                w[s] = v

    def op(self, eng, meth, *args, reads=(), writes=(), inc=True, **kw):
        self._deps(eng, reads, writes)
        if eng == "pe" and not inc:
            self.ops[eng].append(("o", meth, args, kw, None))
            self.pend_r.extend(reads)
            self.pend_w.extend(writes)
            return
        s = self.esem[eng]
        self.val[s] += 1
        ev = (s, self.val[s])
        self.ops[eng].append(("o", meth, args, kw, s))
        rl, wl = list(reads), list(writes)
        if eng == "pe":
            rl += self.pend_r
            wl += self.pend_w
            self.pend_r = []
            self.pend_w = []
        for b in wl:
            b.w = ev
            b.r = {}
        for b in rl:
            if b.r.get(s, 0) < ev[1]:
                b.r[s] = ev[1]

    def dma(self, q, out, in_, reads=(), writes=()):
        pool_ = self.dq[q]
        s = pool_[self.drr[q]]
        self.drr[q] = (self.drr[q] + 1) % len(pool_)
        extra = [(s, self.val[s])] if self.val[s] > 0 else []
        self._deps(q, reads, writes, extra)
        self.val[s] += 16
        ev = (s, self.val[s])
        self.ops[q].append(("d", out, in_, s))
        for b in writes:
            b.w = ev
            b.r = {}
        for b in reads:
            if b.r.get(s, 0) < ev[1]:
                b.r[s] = ev[1]

    def barrier(self):
        assert not self.pend_r and not self.pend_w
        for e in self.names:
            w = self.waited[e]
            for s in range(len(self.sems)):
                if self.val[s] > w.get(s, 0):
                    self.ops[e].append(("w", s, self.val[s]))
                    w[s] = self.val[s]

    def replay(self, name, eng):
        sems = self.sems
        for it in self.ops[name]:
            k = it[0]
            if k == "w":
                eng.wait_ge(sems[it[1]], it[2])
            elif k == "o":
                ins = getattr(eng, it[1])(*it[2], **it[3])
                if it[4] is not None:
                    ins.then_inc(sems[it[4]], 1)
            else:
                eng.dma_start(out=it[1], in_=it[2]).then_inc(sems[it[3]], 16)


class Arena:
    def __init__(self, ap_f32, nbytes):
        self.ap = ap_f32
        self.n = nbytes
        self.off = 0

    def reset(self):
        self.off = 0

    def f32(self, cols):
        o = (self.off + 63) // 64 * 64
        self.off = o + cols * 4
        assert self.off <= self.n, ("arena overflow", self.off)
        return self.ap[:, o // 4:o // 4 + cols]

    def bf16(self, cols):
        o = (self.off + 63) // 64 * 64
        nb = (cols * 2 + 3) // 4 * 4
        self.off = o + nb
        assert self.off <= self.n, ("arena overflow", self.off)
        return self.ap[:, o // 4:o // 4 + nb // 4].bitcast(BF16)[:, 0:cols]


def dap(t, offset, ap):
    return bass.AP(tensor=t.tensor, offset=offset, ap=[list(x) for x in ap])


CF_IDENT, CF_ONES, CF_ONESM, CF_RTA, CF_RTB, CF_J, CF_CMASK, CF_BM, CF_OC3 = (
    0, 128, 256, 384, 512, 640, 704, 768, 768 + 768)
CF_EPS = CF_OC3 + 3
CF_N = CF_OC3 + 4


def host_consts():
    cf = np.zeros((128, CF_N), np.float32)
    cf[:, CF_IDENT:CF_IDENT + 128] = np.eye(128)
    cf[:, CF_ONES:CF_ONES + 128] = 1.0
    cf[:, CF_ONESM:CF_ONESM + 128] = 1.0 / 128.0

    def rt(segs):
        R = np.zeros((128, 128), np.float32)
        for (s0, n) in segs:
            for i in range(n):
                R[s0 + i, s0 + i + n] = -1.0
                R[s0 + i + n, s0 + i] = 1.0
        return R.T.copy()
    cf[:, CF_RTA:CF_RTA + 128] = rt([(0, 32), (64, 32)])
    cf[:, CF_RTB:CF_RTB + 128] = rt([(0, 64)])
    J = np.zeros((64, 64), np.float32)
    for i in range(64):
        J[i, 63 - i] = 1.0
    cf[0:64, CF_J:CF_J + 64] = J
    cm = np.full((64, 64), NEGM, np.float32)
    for qc in range(64):
        sc = min(max(qc - 8, 0), 48)
        cm[sc:sc + 16, qc] = 0.0
    cf[0:64, CF_CMASK:CF_CMASK + 64] = cm
    cf[64:128, CF_CMASK:CF_CMASK + 64] = cm
    j = np.arange(128)[:, None]
    qi = np.arange(128)[None, :]
    mn = np.full((128, 256), NEGM, np.float32)
    mn[:, 0:128][j <= qi] = 0.0
    mn[:, 128:256][j >= qi] = 0.0
    mf = mn.copy(); mf[0:64, :] = NEGM
    ml = mn.copy(); ml[64:128, :] = NEGM
    cf[:, CF_BM:CF_BM + 256] = mn
    cf[:, CF_BM + 256:CF_BM + 512] = mf
    cf[:, CF_BM + 512:CF_BM + 768] = ml
    cf[:, CF_OC3 + 0] = 1.0 / 512.0
    cf[:, CF_OC3 + 1] = 1.0 / 768.0
    cf[:, CF_OC3 + 2] = 1.0 / 768.0
    cf[:, CF_EPS] = EPS
    t = np.arange(SEQ)
    theta = np.float32(10000.0)

    def angles(pos, dim):
        inv = (theta ** (-(np.arange(dim // 2, dtype=np.float32) * np.float32(2.0) / np.float32(dim)))).astype(np.float32)
        return (pos.astype(np.float32)[:, None] * inv[None, :]).astype(np.float32)
    ra = angles(t // 64, 64)
    ca = angles(t % 64, 64)
    ab = angles(t, 128)
    angA = np.concatenate([ra, ra, ca, ca], axis=1)
    angB = np.concatenate([ab, ab], axis=1)
    rope = np.stack([np.cos(angA).T, np.sin(angA).T, np.cos(angB).T, np.sin(angB).T]).astype(np.float32)
    return cf, np.ascontiguousarray(rope)


def build(n_slots=2, layers=(0, 1, 2, 3), phases="PABCOF", debug=False):
    nc = bass.Bass("TRN2", target_bir_lowering=False)
    st = ExitStack()
    with st:
        def din(name, shape):
            return nc.dram_tensor(name, list(shape), F32, kind="ExternalInput").ap()

        skind = "ExternalOutput" if debug else "Internal"

        def dscr(name, shape, dt):
            return nc.dram_tensor(name, list(shape), dt, kind=skind).ap()

        x_in = din("x", (n_slots, SEQ, D_MODEL))
        w_in = din("w_in", (DEPTH, D_MODEL, IN_W))
        w_out = din("w_out", (DEPTH, D_MODEL, D_MODEL))
        w_ff1 = din("w_ff1", (DEPTH, D_MODEL, D_FF))
        w_ff2 = din("w_ff2", (DEPTH, D_FF, D_MODEL))
        aqg = din("a_q_gain", (DEPTH, 128))
        akg = din("a_k_gain", (DEPTH, 128))
        crb = din("c_rel_bias", (DEPTH, 6, 15, 31))
        mixg = din("mix_gain", (DEPTH, D_MODEL))
        ln1g = din("ln1_g", (DEPTH, D_MODEL))
        ln1b = din("ln1_b", (DEPTH, D_MODEL))
        ln2g = din("ln2_g", (DEPTH, D_MODEL))
        ln2b = din("ln2_b", (DEPTH, D_MODEL))
        cf_in = din("cf", (128, CF_N))
        rope_in = din("rope", (4, 128, SEQ))
        y_out = nc.dram_tensor("y", [n_slots, SEQ, D_MODEL], F32, kind="ExternalOutput").ap()

        WBi = nc.dram_tensor("WBi", [DEPTH, D_MODEL, IN_W], BF16, kind="Internal").ap()
        WBo = nc.dram_tensor("WBo", [DEPTH, D_MODEL, D_MODEL], BF16, kind="Internal").ap()
        WB1 = nc.dram_tensor("WB1", [DEPTH, D_MODEL, D_FF], BF16, kind="Internal").ap()
        WB2 = nc.dram_tensor("WB2", [DEPTH, D_FF, D_MODEL], BF16, kind="Internal").ap()
        XS = nc.dram_tensor("XS", [n_slots, SEQ, D_MODEL], F32, kind="Internal").ap()
        X1 = dscr("X1", (SEQ, D_MODEL), F32)
        QT = dscr("QT", (2048, SEQ), BF16)
        KT = dscr("KT", (1792, SEQ), BF16)
        VV = dscr("VV", (SEQ, 1792), BF16)
        MT = dscr("MT", (2048, SEQ), BF16)
        TP = nc.dram_tensor("TP", [DEPTH * 6 * 15, 127], F32, kind="Internal").ap()

        ARENA_B = 203008
        arena_t = st.enter_context(nc.sbuf_tensor("arena", [128, ARENA_B // 4], F32))
        cst_f = st.enter_context(nc.sbuf_tensor("cst_f", [128, CF_N], F32))
        cst_b = st.enter_context(nc.sbuf_tensor("cst_b", [128, CF_N], BF16))
        ps_t = [st.enter_context(nc.psum_tensor("ps%d" % i, [128, 512], F32)) for i in range(8)]
        PS = [t[:] for t in ps_t]
        PSB = [Buf(excl=True) for _ in range(8)]
        s = S(nc, st)
        A = Arena(arena_t[:], ARENA_B)
        CB = Buf()

        ident = cst_b[:, CF_IDENT:CF_IDENT + 128]
        ones_b = cst_b[:, CF_ONES:CF_ONES + 128]
        onesm_b = cst_b[:, CF_ONESM:CF_ONESM + 128]
        rta_b = cst_b[:, CF_RTA:CF_RTA + 128]
        rtb_b = cst_b[:, CF_RTB:CF_RTB + 128]
        jmat = cst_f[0:64, CF_J:CF_J + 64]
        cmask = cst_f[:, CF_CMASK:CF_CMASK + 64]
        bmask = [cst_b[:, CF_BM + 256 * i:CF_BM + 256 * (i + 1)] for i in range(3)]
        oc3 = cst_b[:, CF_OC3:CF_OC3 + 3]
        epsc = cst_f[:, CF_EPS:CF_EPS + 1]

        s.dma("sp", cst_f[:, :], cf_in[:, :], writes=[CB])
        s.op("dve", "tensor_copy", cst_b[:, :], cst_f[:, :], reads=[CB], writes=[CB])
        A.reset()
        zt = A.f32(381)
        zb = Buf()
        s.op("pool", "memset", zt[0:120, :], 0.0, writes=[zb])
        s.dma("sp", TP.rearrange("(p a) y -> p (a y)", a=3), zt[0:120, :], reads=[zb])
        s.barrier()
        s.dma("sp", TP[:, 48:79], crb.rearrange("l h d c -> (l h d) c"))
        s.barrier()

        def cast_weights(src, dst, rows, cols, cw):
            A.reset()
            nb = 3
            ib = [(A.f32(cw), Buf()) for _ in range(nb)]
            ob = [(A.bf16(cw), Buf()) for _ in range(nb)]
            jobs = []
            for l in layers:
                for rb in range(rows // 128):
                    for cb in range(cols // cw):
                        jobs.append((l, rb, cb))
            engs = ["act", "dve", "dve"]

            def load(i):
                l, rb, cb = jobs[i]
                s.dma("sp", ib[i % nb][0], src[l, rb * 128:(rb + 1) * 128, cb * cw:(cb + 1) * cw],
                      writes=[ib[i % nb][1]])
            for i in range(min(nb, len(jobs))):
                load(i)
            for i, (l, rb, cb) in enumerate(jobs):
                e = engs[i % 3]
                if e == "act":
                    s.op("act", "activation", ob[i % nb][0], ib[i % nb][0], AF.Copy,
                         reads=[ib[i % nb][1]], writes=[ob[i % nb][1]])
                else:
                    s.op(e, "tensor_copy", ob[i % nb][0], ib[i % nb][0],
                         reads=[ib[i % nb][1]], writes=[ob[i % nb][1]])
                s.dma("sp", dst[l, rb * 128:(rb + 1) * 128, cb * cw:(cb + 1) * cw], ob[i % nb][0],
                      reads=[ob[i % nb][1]])
                if i + nb < len(jobs):
                    load(i + nb)
            s.barrier()

        if "P" in phases and DBG["cast"]:
            cast_weights(w_in, WBi, D_MODEL, IN_W, 2816)
        if "O" in phases:
            cast_weights(w_out, WBo, D_MODEL, D_MODEL, 2048)
        if "F" in phases:
            cast_weights(w_ff1, WB1, D_MODEL, D_FF, 2048)
            cast_weights(w_ff2, WB2, D_FF, D_MODEL, 2048)

        class XLoader:
            def __init__(self, aux_banks, xin=None, q="sp"):
                self.q = q
                self.xin = xin if xin is not None else [(A.f32(2048), Buf()) for _ in range(2)]
                one = [(A.bf16(2048), Buf()) for _ in range(4)]
                self.xbf = [one, one]
                self.aux = aux_banks
                self.cnt = 0
                self.lc = 0
                self.ec = 0

            def prefetch(self, src_rows_fn, gen):
                for stt in range(4):
                    xi, xib = self.xin[self.lc % 2]
                    self.lc += 1
                    s.dma(self.q, xi, src_rows_fn(stt), writes=[xib])
                    xb, xbb = self.xbf[gen % 2][stt]
                    s.op("dve", "tensor_copy", xb, xi, reads=[xib], writes=[xbb])

            def transpose(self, gen, xT, xTb):
                for kp in range(8):
                    bk = self.aux[self.cnt % len(self.aux)]
                    self.cnt += 1
                    psb = PS[bk].bitcast(BF16)
                    for kk in range(2):
                        kc = 2 * kp + kk
                        for stt in range(4):
                            xb, xbb = self.xbf[gen % 2][stt]
                            last = (kk == 1 and stt == 3)
                            s.op("pe", "transpose", psb[:, kk * 512 + stt * 128:kk * 512 + (stt + 1) * 128],
                                 xb[:, kc * 128:(kc + 1) * 128], ident,
                                 reads=[xbb, CB], writes=[PSB[bk]], inc=last)
                    dst = xT[:, 2 * kp:2 * kp + 2, :].rearrange("p a n -> p (a n)")
                    if self.ec % 2 == 0:
                        s.op("act", "activation", dst, psb, AF.Copy, reads=[PSB[bk]], writes=[xTb])
                    else:
                        s.op("dve", "tensor_copy", dst, psb, reads=[PSB[bk]], writes=[xTb])
                    self.ec += 1

        def ln_epilogue(zb_ap, zb_buf, junk, junkb, stat, statb, g_ap, b_ap, gbuf, dst_rows):
            s.op("dve", "memset", stat[:, 8:16], 0.0, writes=[statb])
            for pc_ in range(4):
                zs = zb_ap[:, pc_ * 512:(pc_ + 1) * 512]
                s.op("act", "activation", junk, zs, AF.Identity, accum_out=stat[:, 8 + pc_:9 + pc_],
                     reads=[zb_buf, statb], writes=[junkb, statb])
                s.op("act", "activation", junk, zs, AF.Square, accum_out=stat[:, 12 + pc_:13 + pc_],
                     reads=[zb_buf, statb], writes=[junkb, statb])
            s.op("dve", "tensor_reduce", stat[:, 0:2], stat[:, 8:16].rearrange("p (a b) -> p a b", b=4),
                 mybir.AxisListType.X, ALU.add, reads=[statb], writes=[statb])
            s.op("dve", "tensor_scalar", stat[:, 2:4], stat[:, 0:2], 1.0 / D_MODEL, None, ALU.mult,
                 reads=[statb], writes=[statb])
            s.op("dve", "tensor_tensor", stat[:, 4:5], stat[:, 2:3], stat[:, 2:3], ALU.mult,
                 reads=[statb], writes=[statb])
            s.op("dve", "tensor_tensor", stat[:, 5:6], stat[:, 3:4], stat[:, 4:5], ALU.subtract,
                 reads=[statb], writes=[statb])
            s.op("act", "activation", stat[:, 6:7], stat[:, 5:6], AF.Sqrt, bias=epsc,
                 reads=[statb, CB], writes=[statb])
            s.op("dve", "reciprocal", stat[:, 6:7], stat[:, 6:7], reads=[statb], writes=[statb])
            s.op("dve", "scalar_tensor_tensor", stat[:, 7:8], stat[:, 2:3], -1.0, stat[:, 6:7],
                 ALU.mult, ALU.mult, reads=[statb], writes=[statb])
            s.op("act", "activation", zb_ap, zb_ap, AF.Identity, bias=stat[:, 7:8], scale=stat[:, 6:7],
                 reads=[zb_buf, statb], writes=[zb_buf])
            s.op("pool", "tensor_tensor", zb_ap, zb_ap, g_ap, ALU.mult, reads=[zb_buf, gbuf], writes=[zb_buf])
            s.op("pool", "tensor_tensor", zb_ap, zb_ap, b_ap, ALU.add, reads=[zb_buf, gbuf], writes=[zb_buf])
            s.dma("pool", dst_rows, zb_ap, reads=[zb_buf])

        def bcast_row(t, l):
            return dap(t, l * D_MODEL, [[0, 128], [1, D_MODEL]])

        def col_vec(t, l, n=128):
            return dap(t, l * n, [[1, n], [1, 1]])

        def phase_P(xsrc, l):
            A.reset()
            xl = XLoader([4, 5, 6, 7])
            xT = [(A.bf16(16 * 512).rearrange("p (k n) -> p k n", n=512), Buf()) for _ in range(2)]
            wb = [(A.bf16(16 * 512).rearrange("p (k n) -> p k n", n=512), Buf()) for _ in range(3)]
            rp = [(A.f32(4 * 512).rearrange("p (k n) -> p k n", n=512), Buf()) for _ in range(2)]
            gq = A.f32(2)
            gb = Buf()
            s.dma("sp", gq[:, 0:1], col_vec(aqg, l), writes=[gb])
            s.dma("sp", gq[:, 1:2], col_vec(akg, l), writes=[gb])
            qbf = [(A.bf16(512), Buf()) for _ in range(2)]
            sqb = [(A.bf16(512), Buf()) for _ in range(2)]
            t1 = [(A.f32(512), Buf()) for _ in range(2)]
            t2 = [(A.f32(512), Buf()) for _ in range(2)]
            rs = [(A.f32(512), Buf()) for _ in range(2)]
            og = [(A.bf16(512), Buf()) for _ in range(4)]
            cnt = {"main": 0, "aux": 0, "w": 0, "e": 0, "og": 0}

            def auxbank():
                b = 4 + cnt["aux"] % 4
                cnt["aux"] += 1
                return b
            pend = [None]

            def flush():
                if pend[0] is not None:
                    f_ = pend[0]
                    pend[0] = None
                    f_()

            def srcfn(tt):
                return lambda stt: xsrc[tt * 512 + stt * 128: tt * 512 + (stt + 1) * 128, :]

            NT = DBG["nt"]
            xl.prefetch(srcfn(0), 0)
            for tt in range(NT):
                xTa, xTb = xT[tt % 2]
                xl.transpose(tt, xTa, xTb)
                if tt + 1 < NT:
                    xl.prefetch(srcfn(tt + 1), tt + 1)
                rpa, rpb = rp[tt % 2]
                s.dma("sp", rpa, rope_in[:, :, tt * 512:(tt + 1) * 512].rearrange("k p n -> p k n"), writes=[rpb])
                for g in range(11):
                    wa, wbb = wb[cnt["w"] % 3]
                    cnt["w"] += 1
                    s.dma("sp", wa, WBi[l, :, g * 512:(g + 1) * 512].rearrange("(k p) n -> p k n", p=128),
                          writes=[wbb])
                    j = 0
                    while j < 4:
                        c = g * 4 + j
                        typ = CH_TYPES[c]
                        kind, di = CH_DEST[c]
                        if (typ[0] if kind != "v" else "V") not in DBG["ptypes"]:
                            j += 1
                            continue
                        if kind == "v":
                            j2 = j
                            while j2 < 4 and CH_DEST[g * 4 + j2][0] == "v":
                                j2 += 1
                            ncol = (j2 - j) * 128
                            for stt in range(4):
                                bk = cnt["main"] % 4
                                cnt["main"] += 1
                                for kc in range(16):
                                    s.op("pe", "matmul", PS[bk][:, 0:ncol], xTa[:, kc, stt * 128:(stt + 1) * 128],
                                         wa[:, kc, j * 128:j2 * 128], start=(kc == 0), stop=(kc == 15),
                                         reads=[xTb, wbb], writes=[PSB[bk]], inc=(kc == 15))
                                oa, ob = og[cnt["og"] % 4]
                                cnt["og"] += 1
                                if cnt["e"] % 2 == 0:
                                    s.op("act", "activation", oa[:, 0:ncol], PS[bk][:, 0:ncol], AF.Copy,
                                         reads=[PSB[bk]], writes=[ob])
                                else:
                                    s.op("dve", "tensor_copy", oa[:, 0:ncol], PS[bk][:, 0:ncol],
                                         reads=[PSB[bk]], writes=[ob])
                                cnt["e"] += 1
                                r0 = tt * 512 + stt * 128
                                s.dma("pool", VV[r0:r0 + 128, di * 128:di * 128 + ncol], oa[:, 0:ncol], reads=[ob])
                                flush()
                            j = j2
                            continue
                        bk = cnt["main"] % 4
                        cnt["main"] += 1
                        for kc in range(16):
                            s.op("pe", "matmul", PS[bk], wa[:, kc, j * 128:(j + 1) * 128], xTa[:, kc, :],
                                 start=(kc == 0), stop=(kc == 15), reads=[xTb, wbb], writes=[PSB[bk]],
                                 inc=(kc == 15))
                        oa, ob = og[cnt["og"] % 4]
                        cnt["og"] += 1
                        dst = (QT if kind == "q" else KT)[di * 128:(di + 1) * 128, tt * 512:(tt + 1) * 512]
                        if typ[0] == "C":
                            s.op("act", "activation", oa, PS[bk], AF.Copy, reads=[PSB[bk]], writes=[ob])
                        elif typ[0] == "B":
                            i2 = cnt["e"] % 2
                            cnt["e"] += 1
                            qa, qb = qbf[i2]
                            s.op("act", "activation", qa, PS[bk], AF.Copy, reads=[PSB[bk]], writes=[qb])

                            def part2(bk=bk, qa=qa, qb=qb, i2=i2, oa=oa, ob=ob, dst=dst, rpa=rpa, rpb=rpb):
                                b2 = auxbank()
                                s.op("pe", "matmul", PS[b2], rtb_b, qa, start=True, stop=True,
                                     reads=[qb, CB], writes=[PSB[b2]])
                                ta, tb = t1[i2]
                                ua, ub = t2[i2]
                                s.op("dve", "tensor_tensor", ta, PS[bk], rpa[:, 2, :], ALU.mult,
                                     reads=[PSB[bk], rpb], writes=[tb])
                                s.op("dve", "tensor_tensor", ua, PS[b2], rpa[:, 3, :], ALU.mult,
                                     reads=[PSB[b2], rpb], writes=[ub])
                                s.op("pool", "tensor_tensor", oa, ta, ua, ALU.add, reads=[tb, ub], writes=[ob])
                                s.dma("pool", dst, oa, reads=[ob])
                            flush()
                            pend[0] = part2
                            j += 1
                            continue
                        else:
                            i2 = cnt["e"] % 2
                            cnt["e"] += 1
                            gcol = gq[:, 0:1] if kind == "q" else gq[:, 1:2]
                            qa, qb = qbf[i2]
                            sa, sb = sqb[i2]
                            s.op("act", "activation", qa, PS[bk], AF.Copy, scale=gcol,
                                 reads=[PSB[bk], gb], writes=[qb])
                            s.op("act", "activation", sa, PS[bk], AF.Square, reads=[PSB[bk]], writes=[sb])

                            def part2(bk=bk, qa=qa, qb=qb, sa=sa, sb=sb, i2=i2, oa=oa, ob=ob, dst=dst, rpa=rpa,
                                      rpb=rpb, gcol=gcol):
                                b2 = auxbank()
                                s.op("pe", "matmul", PS[b2], rta_b, qa, start=True, stop=True,
                                     reads=[qb, CB], writes=[PSB[b2]])
                                b3 = auxbank()
                                s.op("pe", "matmul", PS[b3], onesm_b, sa, start=True, stop=True,
                                     reads=[sb, CB], writes=[PSB[b3]])
                                ta, tb = t1[i2]
                                ua, ub = t2[i2]
                                ra_, rb_ = rs[i2]
                                s.op("dve", "scalar_tensor_tensor", ta, PS[bk], gcol, rpa[:, 0, :], ALU.mult,
                                     ALU.mult, reads=[PSB[bk], rpb, gb], writes=[tb])
                                s.op("dve", "tensor_tensor", ua, PS[b2], rpa[:, 1, :], ALU.mult,
                                     reads=[PSB[b2], rpb], writes=[ub])
                                s.op("act", "activation", ra_, PS[b3], AF.Sqrt, bias=epsc,
                                     reads=[PSB[b3], CB], writes=[rb_])
                                s.op("dve", "reciprocal", ra_, ra_, reads=[rb_], writes=[rb_])
                                s.op("pool", "tensor_tensor", ta, ta, ua, ALU.add, reads=[tb, ub], writes=[tb])
                                s.op("pool", "tensor_tensor", oa, ta, ra_, ALU.mult, reads=[tb, rb_], writes=[ob])
                                s.dma("pool", dst, oa, reads=[ob])
                            flush()
                            pend[0] = part2
                            j += 1
                            continue
                        flush()
                        s.dma("pool", dst, oa, reads=[ob])
                        j += 1
                flush()
            s.barrier()

        def finalize_on(o_bk, l_bk, rl, rlb, oa, ob, dst):
            s.op("dve", "reciprocal", rl, PS[l_bk], reads=[PSB[l_bk]], writes=[rlb])
            s.op("dve", "tensor_tensor", oa, PS[o_bk], rl, ALU.mult, reads=[PSB[o_bk], rlb], writes=[ob])
            s.dma("pool", dst, oa, reads=[ob])

        def phase_A():
            A.reset()
            qa = A.bf16(4 * SEQ).rearrange("p (h t) -> p h t", t=SEQ)
            ka = A.bf16(2 * SEQ).rearrange("p (h t) -> p h t", t=SEQ)
            va = A.bf16(32 * 256).rearrange("p (k c) -> p k c", c=256)
            ib = Buf()
            for h in range(4):
                s.dma("sp", qa[:, h, :], QT[h * 128:(h + 1) * 128, :], writes=[ib])
            for h in range(2):
                s.dma("sp", ka[:, h, :], KT[h * 128:(h + 1) * 128, :], writes=[ib])
            s.dma("sp", va, VV[:, 0:256].rearrange("(k p) c -> p k c", p=128), writes=[ib])
            E = [(A.bf16(512), Buf()) for _ in range(4)]
            rl = [(A.f32(512), Buf()) for _ in range(2)]
            og = [(A.bf16(512), Buf()) for _ in range(2)]
            cnt = 0
            idx = 0
            for h in range(4):
                g = h // 2
                for qt in range(8):
                    o_bk = 4 + idx % 2
                    l_bk = 6 + idx % 2

                    def S_mm(kc, c):
                        bk = c % 4
                        s.op("pe", "matmul", PS[bk], ka[:, g, kc * 128:(kc + 1) * 128],
                             qa[:, h, qt * 512:(qt + 1) * 512], start=True, stop=True,
                             reads=[ib], writes=[PSB[bk]])
                    S_mm(0, cnt)
                    S_mm(1, cnt + 1)
                    for kc in range(32):
                        c = cnt + kc
                        if kc + 2 < 32:
                            S_mm(kc + 2, c + 2)
                        ea, eb = E[c % 4]
                        s.op("act", "activation", ea, PS[c % 4], AF.Exp, scale=SCALE,
                             reads=[PSB[c % 4]], writes=[eb])
                        s.op("pe", "matmul", PS[o_bk], va[:, kc, g * 128:(g + 1) * 128], ea,
                             start=(kc == 0), stop=(kc == 31), reads=[eb, ib], writes=[PSB[o_bk]], inc=False)
                        s.op("pe", "matmul", PS[l_bk], ones_b, ea, start=(kc == 0), stop=(kc == 31),
                             reads=[eb, CB], writes=[PSB[l_bk]], inc=True)
                    cnt += 32
                    finalize_on(o_bk, l_bk, rl[idx % 2][0], rl[idx % 2][1], og[idx % 2][0], og[idx % 2][1],
                                MT[h * 128:(h + 1) * 128, qt * 512:(qt + 1) * 512])
                    idx += 1
            s.barrier()

        def phase_B2():
            for hp in range(2):
                A.reset()
                U = [(A.f32(SEQ), Buf()) for _ in range(3)]
                ls, lsb = A.f32(SEQ), Buf()
                raw = [(A.bf16(SEQ), Buf()) for _ in range(2)]
                E = [(A.bf16(256), Buf()) for _ in range(6)]
                rl = [(A.f32(512), Buf()) for _ in range(2)]
                og = [(A.bf16(512), Buf()) for _ in range(3)]
                cnt = 0
                ev = 0
                for g in range(3):
                    D = B_DIL[g]
                    L = SEQ // D
                    nb = L // 128
                    hb = 2 * g + hp
                    qs = A.bf16(SEQ).rearrange("p (r i) -> p r i", r=D)
                    ks = A.bf16(D * (L + 128)).rearrange("p (r i) -> p r i", r=D)
                    vs = A.bf16(D * (nb + 1) * 128).rearrange("p (r c d) -> p r c d", r=D, d=128)
                    qsb, ksb, vsb = Buf(), Buf(), Buf()
                    ra, rb = raw[0]
                    s.dma("sp", ra, QT[(4 + hb) * 128:(5 + hb) * 128, :], writes=[rb])
                    s.op("dve", "tensor_copy", qs, ra.rearrange("p (i r) -> p r i", r=D), reads=[rb], writes=[qsb])
                    ra, rb = raw[1]
                    s.dma("sp", ra, KT[(2 + hb) * 128:(3 + hb) * 128, :], writes=[rb])
                    s.op("pool", "memset", ks[:, :, 0:64], 0.0, writes=[ksb])
                    s.op("pool", "memset", ks[:, :, L + 64:L + 128], 0.0, writes=[ksb])
                    s.op("dve", "tensor_copy", ks[:, :, 64:64 + L], ra.rearrange("p (i r) -> p r i", r=D),
                         reads=[rb], writes=[ksb])
                    s.op("pool", "memset", vs[0:64, :, 0, :], 0.0, writes=[vsb])
                    s.op("pool", "memset", vs[64:128, :, nb, :], 0.0, writes=[vsb])
                    col0 = 256 + hb * 128
                    for r in range(D):
                        s.dma("sp", vs[64:128, r, 0:nb, :],
                              dap(VV, r * 1792 + col0, [[D * 1792, 64], [128 * D * 1792, nb], [1, 128]]),
                              writes=[vsb])
                        s.dma("sp", vs[0:64, r, 1:nb + 1, :],
                              dap(VV, (64 * D + r) * 1792 + col0, [[D * 1792, 64], [128 * D * 1792, nb], [1, 128]]),
                              writes=[vsb])
                    Ua, Ub = U[g]
                    items = [(r, c) for r in range(D) for c in range(nb + 1)]
                    base = cnt

                    def geom(c):
                        a = 0 if c >= 1 else 128
                        b = 256 if c <= nb - 1 else 128
                        return a, b

                    def S_mm(i):
                        r, c = items[i]
                        a, b = geom(c)
                        qlo = 128 * (c - 1) + a
                        qhi = 128 * (c - 1) + b
                        mk = bmask[1] if c == 0 else (bmask[2] if c == nb else bmask[0])
                        bk = (base + i) % 4
                        s.op("pe", "matmul", PS[bk][:, a:b], ks[:, r, 128 * c:128 * c + 128], qs[:, r, qlo:qhi],
                             start=True, stop=False, reads=[ksb, qsb], writes=[PSB[bk]], inc=False)
                        s.op("pe", "matmul", PS[bk][:, a:b], ident, mk[:, a:b], start=False, stop=True,
                             reads=[CB], writes=[PSB[bk]])
                    S_mm(0)
                    S_mm(1)
                    slot = 0
                    bstart = 0
                    o_bk = l_bk = None
                    for i, (r, c) in enumerate(items):
                        if i + 2 < len(items):
                            S_mm(i + 2)
                        a, b = geom(c)
                        bk = (base + i) % 4
                        ea, eb = E[(base + i) % 6]
                        s.op("act", "activation", ea[:, a:b], PS[bk][:, a:b], AF.Exp, scale=SCALE,
                             reads=[PSB[bk]], writes=[eb])
                        if c >= 1:
                            blk = c - 1
                            if slot == 0:
                                o_bk = 4 + ev % 2
                                l_bk = 6 + ev % 2
                                ev += 1
                                bstart = blk
                            pa, pb = E[(base + i - 1) % 6]
                            cs = slice(slot * 128, (slot + 1) * 128)
                            s.op("pe", "matmul", PS[o_bk][:, cs], vs[:, r, c - 1, :], pa[:, 128:256],
                                 start=True, stop=False, reads=[vsb, pb], writes=[PSB[o_bk]], inc=False)
                            s.op("pe", "matmul", PS[o_bk][:, cs], vs[:, r, c, :], ea[:, 0:128],
                                 start=False, stop=True, reads=[vsb, eb], writes=[PSB[o_bk]], inc=False)
                            s.op("pe", "matmul", PS[l_bk][:, cs], ones_b, pa[:, 128:256],
                                 start=True, stop=False, reads=[CB, pb], writes=[PSB[l_bk]], inc=False)
                            s.op("pe", "matmul", PS[l_bk][:, cs], ones_b, ea[:, 0:128],
                                 start=False, stop=True, reads=[CB, eb], writes=[PSB[l_bk]], inc=True)
                            slot += 1
                            if slot == 4 or blk == nb - 1:
                                n = slot
                                t0 = r + 128 * bstart * D
                                t1_ = t0 + (n * 128 - 1) * D + 1
                                s.op("dve", "tensor_copy", Ua[:, t0:t1_:D], PS[o_bk][:, 0:n * 128],
                                     reads=[PSB[o_bk]], writes=[Ub])
                                if g == 0:
                                    s.op("dve", "tensor_copy", ls[:, t0:t1_:D], PS[l_bk][:, 0:n * 128],
                                         reads=[PSB[l_bk]], writes=[lsb])
                                else:
                                    s.op("dve", "tensor_tensor", ls[:, t0:t1_:D], PS[l_bk][:, 0:n * 128],
                                         ls[:, t0:t1_:D], ALU.add, reads=[PSB[l_bk], lsb], writes=[lsb])
                                slot = 0
                    cnt += len(items)
                for qt in range(8):
                    ra_, rb_ = rl[qt % 2]
                    s.op("dve", "reciprocal", ra_, ls[:, qt * 512:(qt + 1) * 512], reads=[lsb], writes=[rb_])
                    for g in range(3):
                        oa, ob = og[(qt * 3 + g) % 3]
                        hb = 2 * g + hp
                        s.op("pool", "tensor_tensor", oa, U[g][0][:, qt * 512:(qt + 1) * 512], ra_, ALU.mult,
                             reads=[U[g][1], rb_], writes=[ob])
                        s.dma("pool", MT[(4 + hb) * 128:(5 + hb) * 128, qt * 512:(qt + 1) * 512], oa, reads=[ob])
                s.barrier()

        def r0_of(r):
            return min(max(r - 4, 0), 56)

        def phase_C(l):
            A.reset()
            sets = []
            for i in range(2):
                sets.append(dict(
                    qa=A.bf16(SEQ), ka=A.bf16(SEQ),
                    va=A.bf16(32 * 128).rearrange("p (k c) -> p k c", c=128), ib=Buf(),
                    H2=A.f32(15 * 128).rearrange("p (d x) -> p d x", x=128), hb=Buf(),
                    CmR=A.f32(15 * 64).rearrange("p (e q) -> p e q", q=64), cb=Buf(),
                    tiles=[(A.bf16(512), Buf()) for _ in range(20)]))
            E = [(A.bf16(512), Buf()) for _ in range(4)]
            rl = [(A.f32(512), Buf()) for _ in range(2)]
            og = [(A.bf16(512), Buf()) for _ in range(2)]
            state = {"cnt": 0, "mi": 0}

            def prep(h):
                st_ = sets[h % 2]
                qa, ka, va, ib = st_["qa"], st_["ka"], st_["va"], st_["ib"]
                s.dma("sp", qa, QT[(10 + h) * 128:(11 + h) * 128, :], writes=[ib])
                s.dma("sp", ka, KT[(8 + h) * 128:(9 + h) * 128, :], writes=[ib])
                s.dma("sp", va, VV[:, 1024 + h * 128:1024 + (h + 1) * 128].rearrange("(k p) c -> p k c", p=128),
                      writes=[ib])
                H2, hb_, CmR, cb_ = st_["H2"], st_["hb"], st_["CmR"], st_["cb"]
                base = ((l * 6 + h) * 15) * 127
                for half in range(2):
                    s.dma("sp", H2[0:64, :, half * 64:(half + 1) * 64],
                          dap(TP, base, [[1, 64], [127, 15], [1, 64]]), writes=[hb_])
                for dr in range(15):
                    bk = 2 + (dr // 8)
                    sl = slice((dr % 8) * 64, (dr % 8 + 1) * 64)
                    s.op("pe", "matmul", PS[bk][:, sl], H2[0:64, dr, :], jmat, start=True, stop=True,
                         reads=[hb_, CB], writes=[PSB[bk]])
                    s.op("dve", "scalar_tensor_tensor", CmR[:, 14 - dr, :], PS[bk][:, sl], 1.0 / SCALE, cmask,
                         ALU.mult, ALU.add, reads=[PSB[bk], CB], writes=[cb_])
                sigs = {}
                plan = []
                for m in range(8):
                    rows = [8 * m + i for i in range(8)]
                    lo = min(r0_of(r) for r in rows)
                    hi = max(r0_of(r) + 7 for r in rows)
                    js = list(range(lo // 2, hi // 2 + 1))
                    for n, j in enumerate(js):
                        sig = []
                        for arel in range(2):
                            a_ = 2 * j + arel
                            val = [i for i, r in enumerate(rows) if r0_of(r) <= a_ <= r0_of(r) + 7]
                            if val:
                                rlo, rhi = val[0], val[-1] + 1
                                assert val == list(range(rlo, rhi))
                                elo = 7 - a_ + rows[rlo]
                                assert 0 <= elo and elo + (rhi - rlo) <= 15
                                sig.append((rlo, rhi, elo))
                            else:
                                sig.append(None)
                        sig = tuple(sig)
                        if sig not in sigs:
                            ti = len(sigs)
                            ta, tb = st_["tiles"][ti]
                            s.op("pool", "memset", ta, NEGM, writes=[tb])
                            for arel in range(2):
                                if sig[arel] is None:
                                    continue
                                rlo, rhi, elo = sig[arel]
                                pr = slice(arel * 64, (arel + 1) * 64)
                                s.op("pool", "tensor_copy",
                                     ta[pr, rlo * 64:rhi * 64].rearrange("p (n q) -> p n q", q=64),
                                     CmR[pr, elo:elo + (rhi - rlo), :], reads=[cb_], writes=[tb])
                            sigs[sig] = ti
                        plan.append((m, n, len(js), j, sigs[sig]))
                return plan

            def main(h, plan):
                st_ = sets[h % 2]
                qa, ka, va, ib = st_["qa"], st_["ka"], st_["va"], st_["ib"]
                base = state["cnt"]

                def S_mm(i):
                    m, n, nn, j, ti = plan[i]
                    bk = (base + i) % 4
                    ta, tb = st_["tiles"][ti]
                    s.op("pe", "matmul", PS[bk], ka[:, j * 128:(j + 1) * 128], qa[:, m * 512:(m + 1) * 512],
                         start=True, stop=False, reads=[ib], writes=[PSB[bk]], inc=False)
                    s.op("pe", "matmul", PS[bk], ident, ta, start=False, stop=True,
                         reads=[CB, tb], writes=[PSB[bk]])
                S_mm(0)
                S_mm(1)
                for i, (m, n, nn, j, ti) in enumerate(plan):
                    c = base + i
                    if i + 2 < len(plan):
                        S_mm(i + 2)
                    if n == 0:
                        state["mi"] += 1
                    o_bk = 4 + state["mi"] % 2
                    l_bk = 6 + state["mi"] % 2
                    ea, eb = E[c % 4]
                    s.op("act", "activation", ea, PS[c % 4], AF.Exp, scale=SCALE, reads=[PSB[c % 4]], writes=[eb])
                    s.op("pe", "matmul", PS[o_bk], va[:, j, :], ea, start=(n == 0), stop=(n == nn - 1),
                         reads=[ib, eb], writes=[PSB[o_bk]], inc=False)
                    s.op("pe", "matmul", PS[l_bk], ones_b, ea, start=(n == 0), stop=(n == nn - 1),
                         reads=[CB, eb], writes=[PSB[l_bk]], inc=True)
                    if n == nn - 1:
                        k2 = state["mi"] % 2
                        finalize_on(o_bk, l_bk, rl[k2][0], rl[k2][1], og[k2][0], og[k2][1],
                                    MT[(10 + h) * 128:(11 + h) * 128, m * 512:(m + 1) * 512])
                state["cnt"] += len(plan)

            plans = {0: prep(0)}
            for h in range(6):
                if h + 1 < 6:
                    plans[h + 1] = prep(h + 1)
                main(h, plans[h])
            s.barrier()

        GRP = [(0, 4), (4, 10), (10, 16)]

        def phase_O(xsrc, l):
            A.reset()
            W = A.bf16(16 * 2048).rearrange("p (k n) -> p k n", n=2048)
            Wb = Buf()
            for q4 in range(4):
                s.dma("sp", W[:, 4 * q4:4 * q4 + 4, :],
                      WBo[l, 512 * q4:512 * (q4 + 1), :].rearrange("(k p) n -> p k n", p=128), writes=[Wb])
            mg = A.f32(16)
            mgb = Buf()
            s.dma("sp", mg, dap(mixg, l * D_MODEL, [[1, 128], [128, 16]]), writes=[mgb])
            for kc in range(16):
                s.op("dve", "tensor_scalar", W[:, kc, :], W[:, kc, :], mg[:, kc:kc + 1], None,
                     ALU.mult, reads=[Wb, mgb], writes=[Wb])
            gt = A.f32(2048)
            bt = A.f32(2048)
            gbuf = Buf()
            s.dma("sp", gt, bcast_row(ln1g, l), writes=[gbuf])
            s.dma("sp", bt, bcast_row(ln1b, l), writes=[gbuf])
            mt = [(A.bf16(16 * 512).rearrange("p (k n) -> p k n", n=512), Buf()) for _ in range(2)]
            sq = (A.bf16(16 * 512).rearrange("p (k n) -> p k n", n=512), Buf())
            xb = [(A.f32(2048), Buf()) for _ in range(5)]
            junk, junkb = A.bf16(512), Buf()
            stat = [(A.f32(16), Buf()) for _ in range(5)]
            r3 = [(A.f32(4), Buf()) for _ in range(5)]
            NT = SEQ // 512
            cnt = 0
            bkc = 0
            pend_ln = None
            for tt in range(NT):
                ma, mb = mt[tt % 2]
                s.dma("sp", ma, MT[:, tt * 512:(tt + 1) * 512].rearrange("(k p) n -> p k n", p=128), writes=[mb])
                s.op("act", "activation", sq[0].rearrange("p k n -> p (k n)"), ma.rearrange("p k n -> p (k n)"),
                     AF.Square, reads=[mb], writes=[sq[1]])
                for stt in range(4):
                    xa, xbb = xb[cnt % 5]
                    sa, sb = stat[cnt % 5]
                    ra_, rb_ = r3[cnt % 5]
                    cnt += 1
                    r0 = tt * 512 + stt * 128
                    s.dma("sp", xa, xsrc[r0:r0 + 128, :], writes=[xbb])
                    s.op("act", "activation", xa, xa, AF.Copy, scale=ALPHA, reads=[xbb], writes=[xbb])
                    for gi, (k0, k1) in enumerate(GRP):
                        for kc in range(k0, k1):
                            s.op("pe", "matmul", PS[7][:, gi:gi + 1], sq[0][:, kc, stt * 128:(stt + 1) * 128],
                                 oc3[:, gi:gi + 1], start=(kc == k0), stop=(kc == k1 - 1),
                                 reads=[sq[1], CB], writes=[PSB[7]], inc=(kc == k1 - 1))
                    s.op("act", "activation", ra_[:, 0:3], PS[7][:, 0:3], AF.Sqrt, bias=epsc,
                         reads=[PSB[7], CB], writes=[rb_])
                    s.op("dve", "reciprocal", ra_[:, 0:3], ra_[:, 0:3], reads=[rb_], writes=[rb_])
                    for cg in range(4):
                        cs = slice(cg * 512, (cg + 1) * 512)
                        bks = []
                        for gi, (k0, k1) in enumerate(GRP):
                            bk = bkc % 6
                            bkc += 1
                            bks.append(bk)
                            for kc in range(k0, k1):
                                s.op("pe", "matmul", PS[bk], ma[:, kc, stt * 128:(stt + 1) * 128], W[:, kc, cs],
                                     start=(kc == k0), stop=(kc == k1 - 1), reads=[mb, Wb], writes=[PSB[bk]],
                                     inc=(kc == k1 - 1))
                        for gi in range(3):
                            s.op("dve", "scalar_tensor_tensor", xa[:, cs], PS[bks[gi]], ra_[:, gi:gi + 1], xa[:, cs],
                                 ALU.mult, ALU.add, reads=[PSB[bks[gi]], rb_, xbb], writes=[xbb])
                    if pend_ln is not None:
                        ln_epilogue(*pend_ln)
                    pend_ln = (xa, xbb, junk, junkb, sa, sb, gt, bt, gbuf, X1[r0:r0 + 128, :])
            if pend_ln is not None:
                ln_epilogue(*pend_ln)
            s.barrier()

        def phase_F(xdst, l):
            A.reset()
            xb = [(A.f32(2048), Buf()) for _ in range(4)]
            xl = XLoader([0, 1, 2, 3, 4, 5, 6, 7], xin=xb[0:2], q="pool")
            xT = (A.bf16(16 * 512).rearrange("p (k n) -> p k n", n=512), Buf())
            hT = (A.bf16(64 * 512).rearrange("p (k n) -> p k n", n=512), Buf())
            wb = [(A.bf16(16 * 512).rearrange("p (k n) -> p k n", n=512), Buf()) for _ in range(3)]
            rt_ = [(A.f32(512), Buf()) for _ in range(2)]
            gt = A.f32(2048)
            bt = A.f32(2048)
            gbuf = Buf()
            s.dma("sp", gt, bcast_row(ln2g, l), writes=[gbuf])
            s.dma("sp", bt, bcast_row(ln2b, l), writes=[gbuf])
            junk, junkb = A.bf16(512), Buf()
            stat = [(A.f32(16), Buf()) for _ in range(4)]
            wc = 0
            pc = 0
            rc = 0

            def srcfn(tt):
                return lambda stt: X1[tt * 512 + stt * 128: tt * 512 + (stt + 1) * 128, :]
            NT = SEQ // 512
            pend_f = []
            xl.prefetch(srcfn(0), 0)
            for tt in range(NT):
                xl.transpose(tt, xT[0], xT[1])
                for fg in range(16):
                    if 1 <= fg <= 4 and pend_f:
                        ln_epilogue(*pend_f.pop(0))
                    if fg == 6 and tt + 1 < NT:
                        xl.prefetch(srcfn(tt + 1), tt + 1)
                    wa, wbb = wb[wc % 3]
                    wc += 1
                    s.dma("sp", wa, WB1[l, :, fg * 512:(fg + 1) * 512].rearrange("(k p) n -> p k n", p=128),
                          writes=[wbb])
                    for j in range(4):
                        fc = fg * 4 + j
                        bk = pc % 8
                        pc += 1
                        for kc in range(16):
                            s.op("pe", "matmul", PS[bk], wa[:, kc, j * 128:(j + 1) * 128], xT[0][:, kc, :],
                                 start=(kc == 0), stop=(kc == 15), reads=[wbb, xT[1]], writes=[PSB[bk]],
                                 inc=(kc == 15))
                        ra_, rb_ = rt_[rc % 2]
                        rc += 1
                        s.op("act", "activation", ra_, PS[bk], AF.Relu, reads=[PSB[bk]], writes=[rb_])
                        s.op("dve", "tensor_tensor", hT[0][:, fc, :], ra_, ra_, ALU.mult, reads=[rb_], writes=[hT[1]])
                def reload_residual():
                    for stt in range(4):
                        xa, xbb = xb[stt]
                        r0 = tt * 512 + stt * 128
                        s.dma("sp", xa, X1[r0:r0 + 128, :], writes=[xbb])
                        s.op("act", "activation", xa, xa, AF.Copy, scale=ALPHA, reads=[xbb], writes=[xbb])
                for cg in range(4):
                    cs = slice(cg * 512, (cg + 1) * 512)
                    bset = [(cg % 2) * 4 + i for i in range(4)]
                    for piece in range(4):
                        wa, wbb = wb[wc % 3]
                        wc += 1
                        s.dma("sp", wa, WB2[l, piece * 2048:(piece + 1) * 2048, cs].rearrange("(k p) n -> p k n", p=128),
                              writes=[wbb])
                        for stt in range(4):
                            for kc in range(16):
                                fc = piece * 16 + kc
                                s.op("pe", "matmul", PS[bset[stt]], hT[0][:, fc, stt * 128:(stt + 1) * 128],
                                     wa[:, kc, :], start=(fc == 0), stop=(fc == 63), reads=[hT[1], wbb],
                                     writes=[PSB[bset[stt]]], inc=(kc == 15))
                        if cg == 0 and piece == 1:
                            reload_residual()
                    for stt in range(4):
                        xa, xbb = xb[stt]
                        s.op("dve", "tensor_tensor", xa[:, cs], PS[bset[stt]], xa[:, cs], ALU.add,
                             reads=[PSB[bset[stt]], xbb], writes=[xbb])
                pc = 0
                for stt in range(4):
                    xa, xbb = xb[stt]
                    r0 = tt * 512 + stt * 128
                    pend_f.append((xa, xbb, junk, junkb, stat[stt][0], stat[stt][1], gt, bt, gbuf,
                                   xdst[r0:r0 + 128, :]))
            while pend_f:
                ln_epilogue(*pend_f.pop(0))
            s.barrier()

        nl = len(layers)
        for slot in range(n_slots):
            for li, l in enumerate(layers):
                xsrc = x_in[slot] if li == 0 else XS[slot]
                xdst = y_out[slot] if li == nl - 1 else XS[slot]
                if "P" in phases:
                    phase_P(xsrc, l)
                if "A" in phases:
                    phase_A()
                if "B" in phases:
                    phase_B2()
                if "C" in phases:
                    phase_C(l)
                if "O" in phases:
                    phase_O(xsrc, l)
                if "F" in phases:
                    phase_F(xdst, l)
        s.barrier()

        with nc.allow_non_contiguous_dma("tiny per-layer parameter vectors"):
            with nc.Block() as block:
                @block.tensor
                def _(e):
                    s.replay("pe", e)

                @block.scalar
                def _(e):
                    s.replay("act", e)

                @block.vector
                def _(e):
                    s.replay("dve", e)

                @block.gpsimd
                def _(e):
                    s.replay("pool", e)

                @block.sync
                def _(e):
                    s.replay("sp", e)
    return nc


_NC_CACHE = {}


def kernel(x_prompt, x_sample, w_in, a_q_gain, a_k_gain, c_rel_bias, mix_gain, w_out,
           ln1_g, ln1_b, w_ff1, w_ff2, ln2_g, ln2_b):
    f = lambda a: np.ascontiguousarray(np.asarray(a, dtype=np.float32))
    xp, xs = f(x_prompt), f(x_sample)
    seqs = [xp[i] for i in range(4)] + [xs[i] for i in range(8)]
    slot_seq = []
    for c in range(8):
        if c < 4:
            slot_seq.append((c, 8 + c))
        else:
            slot_seq.append((c, c))
    cf, rope = host_consts()
    shared = dict(w_in=f(w_in), w_out=f(w_out), w_ff1=f(w_ff1), w_ff2=f(w_ff2),
                  a_q_gain=f(a_q_gain), a_k_gain=f(a_k_gain), c_rel_bias=f(c_rel_bias),
                  mix_gain=f(mix_gain), ln1_g=f(ln1_g), ln1_b=f(ln1_b), ln2_g=f(ln2_g), ln2_b=f(ln2_b),
                  cf=cf, rope=rope)
    in_maps = []
    for c in range(8):
        m = dict(shared)
        m["x"] = np.stack([seqs[slot_seq[c][0]], seqs[slot_seq[c][1]]])
        in_maps.append(m)
    if "nc" not in _NC_CACHE:
        _NC_CACHE["nc"] = build()
    res = run_bass_kernel_spmd(_NC_CACHE["nc"], in_maps, core_ids=list(range(8)))
    outs = [None] * 12
    for c in range(8):
        y = res.results[c]["y"]
        outs[slot_seq[c][0]] = y[0]
        if c < 4:
            outs[slot_seq[c][1]] = y[1]
    y_prompt = np.stack(outs[0:4]).astype(np.float32)
    y_sample = np.stack(outs[4:12]).astype(np.float32)
    return (y_prompt, y_sample)
```
